# Optimizing a Trainium2 kernel written in Bass

```python
import jax, jax.numpy as jnp
from jax import lax
import numpy as np

D_MODEL = 2048
BATCH = 4
SEQ = 4096
DEPTH = 2

CHUNK = 64
N_HEADS_A = 8
HEAD_DIM_A = 128
D_ATTN = N_HEADS_A * HEAD_DIM_A
LEFT_CHUNKS = 8
BAND = (LEFT_CHUNKS + 1) * CHUNK
MAX_REL = 4 * CHUNK
N_REL = MAX_REL + CHUNK
N_HEADS_M = 4
HEAD_DIM_M = 256
D_MLSTM = N_HEADS_M * HEAD_DIM_M
CONV_W = 4
D_FF = ((8 * D_MODEL // 3 + 127) // 128) * 128
N_BRANCH = 2
D_IN = 3 * D_ATTN + 4 * D_MLSTM + 2 * N_HEADS_M + N_BRANCH * D_MODEL
EPS = 1e-6

kernel_name = 'hybrid_chunk_attn_mlstm_macaron_adaln'


def rmsnorm(x, g):
    xf = x.astype(jnp.float32)
    y = xf * lax.rsqrt(jnp.mean(xf * xf, axis=-1, keepdims=True) + EPS)
    return (y * g.astype(jnp.float32)).astype(x.dtype)


def modulate(x, g, shift, scale):
    return rmsnorm(x, g) * (1 + scale[:, None, :]) + shift[:, None, :]


def swiglu(h, w1, w3, w2):
    return (jax.nn.silu(h @ w1) * (h @ w3)) @ w2


def causal_dwconv(u, w, b):
    out = lax.conv_general_dilated(
        u, w[:, None, :], window_strides=(1,), padding=((CONV_W - 1, 0),),
        dimension_numbers=('NWC', 'WIO', 'NWC'), feature_group_count=u.shape[-1])
    return out + b


def rel_bias_matrix(table):
    qi = np.arange(CHUNK)[:, None]
    kj = np.arange(BAND)[None, :]
    rel = LEFT_CHUNKS * CHUNK + qi - kj
    idx = np.clip(rel, -(CHUNK - 1), MAX_REL) + (CHUNK - 1)
    return table[:, idx]


def chunk_attention(q, k, v, q_g, k_g, table):
    B, S, _ = q.shape
    NC = S // CHUNK
    shp = (B, NC, CHUNK, N_HEADS_A, HEAD_DIM_A)
    q = rmsnorm(q.reshape(shp), q_g)
    k = rmsnorm(k.reshape(shp), k_g)
    v = v.reshape(shp)
    pad = ((0, 0), (LEFT_CHUNKS, 0), (0, 0), (0, 0), (0, 0))
    kp = jnp.pad(k, pad)
    vp = jnp.pad(v, pad)
    k_band = jnp.concatenate([kp[:, j:j + NC] for j in range(LEFT_CHUNKS + 1)], axis=2)
    v_band = jnp.concatenate([vp[:, j:j + NC] for j in range(LEFT_CHUNKS + 1)], axis=2)
    s = jnp.einsum('bcqhd,bckhd->bhcqk', q, k_band).astype(jnp.float32) * (HEAD_DIM_A ** -0.5)
    s = s + rel_bias_matrix(table).astype(jnp.float32)[None, :, None]
    valid = (np.arange(NC)[:, None] - LEFT_CHUNKS + np.arange(BAND)[None, :] // CHUNK) >= 0
    s = jnp.where(valid[None, None, :, None, :], s, -jnp.inf)
    p = jax.nn.softmax(s, axis=-1).astype(v.dtype)
    o = jnp.einsum('bhcqk,bckhd->bcqhd', p, v_band)
    return o.reshape(B, S, D_ATTN)


def mlstm_chunkwise(q, k, v, i_pre, f_pre):
    B, S, _ = q.shape
    NC = S // CHUNK
    f32 = jnp.float32

    def heads(t):
        return t.astype(f32).reshape(B, NC, CHUNK, N_HEADS_M, HEAD_DIM_M).transpose(0, 3, 1, 2, 4)

    def gates(t):
        return t.astype(f32).reshape(B, NC, CHUNK, N_HEADS_M).transpose(0, 3, 1, 2)

    q = heads(q)
    k = heads(k) * (HEAD_DIM_M ** -0.5)
    v = heads(v)
    ig = gates(i_pre)
    b = jnp.cumsum(jax.nn.log_sigmoid(gates(f_pre)), axis=-1)
    b_last = b[..., -1]
    causal = np.tril(np.ones((CHUNK, CHUNK), dtype=bool))
    d = jnp.where(causal, b[..., :, None] - b[..., None, :] + ig[..., None, :], -jnp.inf)

    w_end = b_last[..., None] - b + ig
    g_end = jnp.max(w_end, axis=-1)
    kw = k * jnp.exp(w_end - g_end[..., None])[..., None]
    a_c = jnp.einsum('bhcsk,bhcsv->bhckv', kw, v)
    a_n = jnp.sum(kw, axis=-2)

    def step(carry, inp):
        C, n, m = carry
        bl, gc, ac, an = inp
        m_new = jnp.maximum(bl + m, gc)
        decay = jnp.exp(bl + m - m_new)
        inj = jnp.exp(gc - m_new)
        C_new = decay[..., None, None] * C + inj[..., None, None] * ac
        n_new = decay[..., None] * n + inj[..., None] * an
        return (C_new, n_new, m_new), (C, n, m)

    init = (jnp.zeros((B, N_HEADS_M, HEAD_DIM_M, HEAD_DIM_M), f32),
            jnp.zeros((B, N_HEADS_M, HEAD_DIM_M), f32),
            jnp.zeros((B, N_HEADS_M), f32))
    xs = (jnp.moveaxis(b_last, 2, 0), jnp.moveaxis(g_end, 2, 0),
          jnp.moveaxis(a_c, 2, 0), jnp.moveaxis(a_n, 2, 0))
    _, (c_prev, n_prev, m_prev) = lax.scan(step, init, xs)
    c_prev = jnp.moveaxis(c_prev, 0, 2)
    n_prev = jnp.moveaxis(n_prev, 0, 2)
    m_prev = jnp.moveaxis(m_prev, 0, 2)

    m_inter = b + m_prev[..., None]
    m_t = jnp.maximum(m_inter, jnp.max(d, axis=-1))
    s = jnp.exp(d - m_t[..., None]) * jnp.einsum('bhctk,bhcsk->bhcts', q, k)
    w_inter = jnp.exp(m_inter - m_t)
    num = (jnp.einsum('bhcts,bhcsv->bhctv', s, v)
           + w_inter[..., None] * jnp.einsum('bhctk,bhckv->bhctv', q, c_prev))
    den = jnp.sum(s, axis=-1) + w_inter * jnp.einsum('bhctk,bhck->bhct', q, n_prev)
    h = num / jnp.maximum(jnp.abs(den), jnp.exp(-m_t))[..., None]
    return h.transpose(0, 2, 3, 1, 4).reshape(B, S, N_HEADS_M, HEAD_DIM_M)


def token_mixer(h, w_in, b_if, conv_w, conv_b, q_norm_g, k_norm_g, rel_table, m_norm_g,
                w_up_a, w_up_m, w_out):
    B, S, _ = h.shape
    z = h @ w_in
    offs = np.cumsum([D_ATTN, D_ATTN, D_ATTN, 2 * D_MLSTM, D_MLSTM, D_MLSTM,
                      2 * N_HEADS_M, D_MODEL]).tolist()
    q_a, k_a, v_a, qk_m, v_m, o_m, if_m, gate_a, gate_m = jnp.split(z, offs, axis=-1)
    attn = chunk_attention(q_a, k_a, v_a, q_norm_g, k_norm_g, rel_table)
    qk_m = jax.nn.silu(causal_dwconv(qk_m, conv_w, conv_b))
    q_m, k_m = jnp.split(qk_m, 2, axis=-1)
    if_m = if_m + b_if
    hm = mlstm_chunkwise(q_m, k_m, v_m, if_m[..., :N_HEADS_M], if_m[..., N_HEADS_M:])
    hm = rmsnorm(hm, m_norm_g.reshape(N_HEADS_M, HEAD_DIM_M)).reshape(B, S, D_MLSTM).astype(h.dtype)
    hm = jax.nn.sigmoid(o_m) * hm
    merged = jax.nn.sigmoid(gate_a) * (attn @ w_up_a) + jax.nn.sigmoid(gate_m) * (hm @ w_up_m)
    return merged @ w_out


def setup_inputs(seed: int = 0) -> dict:
    key = jax.random.key(seed)
    ks = jax.random.split(key, 23)
    f32 = jnp.float32

    def nrm(k, shape, fan_in, scale=1.0):
        return jax.random.normal(k, shape, f32) * (scale * fan_in ** -0.5)

    def gain(k, shape):
        return 1.0 + 0.05 * jax.random.normal(k, shape, f32)

    x = jax.random.normal(ks[0], (BATCH, SEQ, D_MODEL), f32)
    c = jax.random.normal(ks[1], (BATCH, D_MODEL), f32)
    norm_g = gain(ks[2], (DEPTH, 3, D_MODEL))
    w_ada = nrm(ks[3], (DEPTH, D_MODEL, 9 * D_MODEL), D_MODEL, 0.5)
    b_ada = 0.02 * jax.random.normal(ks[4], (DEPTH, 9 * D_MODEL), f32)
    ffn1_w1 = nrm(ks[5], (DEPTH, D_MODEL, D_FF), D_MODEL)
    ffn1_w3 = nrm(ks[6], (DEPTH, D_MODEL, D_FF), D_MODEL)
    ffn1_w2 = nrm(ks[7], (DEPTH, D_FF, D_MODEL), D_FF)
    w_in = nrm(ks[8], (DEPTH, D_MODEL, D_IN), D_MODEL)
    b_i = 0.1 * jax.random.normal(ks[9], (DEPTH, N_HEADS_M), f32)
    b_f = jnp.linspace(3.0, 6.0, N_HEADS_M, dtype=f32)[None, :] + 0.1 * jax.random.normal(ks[10], (DEPTH, N_HEADS_M), f32)
    b_if = jnp.concatenate([b_i, b_f], axis=-1)
    conv_w = nrm(ks[11], (DEPTH, CONV_W, 2 * D_MLSTM), CONV_W)
    conv_b = 0.02 * jax.random.normal(ks[12], (DEPTH, 2 * D_MLSTM), f32)
    q_norm_g = gain(ks[13], (DEPTH, HEAD_DIM_A))
    k_norm_g = gain(ks[14], (DEPTH, HEAD_DIM_A))
    rel_table = 0.2 * jax.random.normal(ks[15], (DEPTH, N_HEADS_A, N_REL), f32)
    m_norm_g = gain(ks[16], (DEPTH, D_MLSTM))
    w_up_a = nrm(ks[17], (DEPTH, D_ATTN, D_MODEL), D_ATTN)
    w_up_m = nrm(ks[18], (DEPTH, D_MLSTM, D_MODEL), D_MLSTM)
    w_out = nrm(ks[19], (DEPTH, D_MODEL, D_MODEL), D_MODEL)
    ffn2_w1 = nrm(ks[20], (DEPTH, D_MODEL, D_FF), D_MODEL)
    ffn2_w3 = nrm(ks[21], (DEPTH, D_MODEL, D_FF), D_MODEL)
    ffn2_w2 = nrm(ks[22], (DEPTH, D_FF, D_MODEL), D_FF)
    return {'x': x, 'c': c, 'norm_g': norm_g, 'w_ada': w_ada, 'b_ada': b_ada,
            'ffn1_w1': ffn1_w1, 'ffn1_w3': ffn1_w3, 'ffn1_w2': ffn1_w2,
            'w_in': w_in, 'b_if': b_if, 'conv_w': conv_w, 'conv_b': conv_b,
            'q_norm_g': q_norm_g, 'k_norm_g': k_norm_g, 'rel_table': rel_table,
            'm_norm_g': m_norm_g, 'w_up_a': w_up_a, 'w_up_m': w_up_m, 'w_out': w_out,
            'ffn2_w1': ffn2_w1, 'ffn2_w3': ffn2_w3, 'ffn2_w2': ffn2_w2}


def reference(x, c, norm_g, w_ada, b_ada, ffn1_w1, ffn1_w3, ffn1_w2, w_in, b_if, conv_w, conv_b,
              q_norm_g, k_norm_g, rel_table, m_norm_g, w_up_a, w_up_m, w_out,
              ffn2_w1, ffn2_w3, ffn2_w2):
    c_act = jax.nn.silu(c)
    for l in range(DEPTH):
        mod = c_act @ w_ada[l] + b_ada[l]
        sh1, sc1, g1, sh2, sc2, g2, sh3, sc3, g3 = jnp.split(mod, 9, axis=-1)
        x = x + 0.5 * g1[:, None, :] * swiglu(modulate(x, norm_g[l, 0], sh1, sc1),
                                              ffn1_w1[l], ffn1_w3[l], ffn1_w2[l])
        x = x + g2[:, None, :] * token_mixer(modulate(x, norm_g[l, 1], sh2, sc2),
                                             w_in[l], b_if[l], conv_w[l], conv_b[l],
                                             q_norm_g[l], k_norm_g[l], rel_table[l], m_norm_g[l],
                                             w_up_a[l], w_up_m[l], w_out[l])
        x = x + 0.5 * g3[:, None, :] * swiglu(modulate(x, norm_g[l, 2], sh3, sc3),
                                              ffn2_w1[l], ffn2_w3[l], ffn2_w2[l])
    return x
```

```python
import contextlib
import numpy as np
import concourse.bass as bass
import concourse.mybir as mybir
from concourse.bass_utils import run_bass_kernel_spmd

F32 = mybir.dt.float32
BF16 = mybir.dt.bfloat16
AF = mybir.ActivationFunctionType
ALU = mybir.AluOpType

T = 4096
D = 2048
KC = 16
FF = 5504
FC = 43
DIN = 11272
HA = 8
HM = 4
DEPTH = 2
EPS = 1e-6
NB = T // 512
SB_BASE = 16512
SB_END = 229376
ENGS = ("pe", "act", "dve", "pool", "sp")


class Buf:
    __slots__ = ("name", "writers", "readers", "dslot", "epoch")

    def __init__(self, name):
        self.name = name
        self.writers = []
        self.readers = []
        self.dslot = None
        self.epoch = -1


class SemSlot:
    __slots__ = ("sem", "count")

    def __init__(self):
        self.sem = None
        self.count = 0


class Op:
    __slots__ = ("eng", "fn", "deps", "hard", "is_dma", "dbuf", "dslot", "milestone", "dma_wait_vals")

    def __init__(self, eng, fn):
        self.eng = eng
        self.fn = fn
        self.deps = set()
        self.hard = set()
        self.is_dma = False
        self.dbuf = None
        self.dslot = None
        self.milestone = None
        self.dma_wait_vals = {}


class Prog:
    def __init__(self, nc):
        self.nc = nc
        self.ops = []
        self.bar = None
        self.slots = []
        self.slot_i = 0
        self.epoch = 0

    def _add(self, op, reads, writes, pwrites):
        oid = len(self.ops)
        if self.bar is not None:
            op.deps.add(self.bar)
        for b in reads:
            op.deps.update(b.writers)
            op.hard.update(b.writers)
        for b in writes:
            op.deps.update(b.writers)
            op.hard.update(b.writers)
            op.deps.update(b.readers)
        for b in pwrites:
            op.deps.update(b.readers)
        for d in op.deps:
            dop = self.ops[d]
            if dop.is_dma:
                op.dma_wait_vals[d] = dop.dslot.count
        self.ops.append(op)
        for b in reads:
            b.readers.append(oid)
        for b in writes:
            b.writers = [oid]
            b.readers = []
        for b in pwrites:
            b.writers.append(oid)
        return oid

    def op(self, eng, fn, reads=(), writes=(), pwrites=()):
        return self._add(Op(eng, fn), reads, writes, pwrites)

    def dma(self, eng, fn, sbuf, reads=(), writes=(), pwrites=(), n=1):
        o = Op(eng, fn)
        o.is_dma = True
        o.dbuf = sbuf
        if sbuf.dslot is None or sbuf.epoch != self.epoch:
            if self.slot_i >= len(self.slots):
                self.slots.append(SemSlot())
            sbuf.dslot = self.slots[self.slot_i]
            self.slot_i += 1
            sbuf.epoch = self.epoch
        o.dslot = sbuf.dslot
        oid = self._add(o, reads, writes, pwrites)
        o.dslot.count += n
        return oid

    def barrier(self, bufs, newbufs=()):
        oid = self.op("sp", None, writes=list(bufs))
        self.bar = oid
        self.epoch += 1
        self.slot_i = 0
        return oid

    def emit(self):
        nc = self.nc
        ops = self.ops
        need_ms = [False] * len(ops)
        for o in ops:
            for d in o.deps:
                dop = ops[d]
                if (not dop.is_dma) and (dop.eng != o.eng or (o.eng != "pe" and d in o.hard)):
                    need_ms[d] = True
        counters = {e: 0 for e in ENGS}
        for i, o in enumerate(ops):
            if need_ms[i]:
                counters[o.eng] += 1
                o.milestone = counters[o.eng]
        stack = contextlib.ExitStack()
        with stack:
            esem = {e: stack.enter_context(nc.semaphore("ms_" + e)) for e in ENGS}
            nd = len(self.slots)
            for si, sl_ in enumerate(self.slots):
                sl_.sem = stack.enter_context(nc.semaphore("dsem%d" % si))
            per_eng = {e: [] for e in ENGS}
            for i, o in enumerate(ops):
                per_eng[o.eng].append(i)
            block = stack.enter_context(nc.Block())

            def run_engine(ename, eobj):
                seen = {}
                for i in per_eng[ename]:
                    o = ops[i]
                    w = {}
                    for d in o.deps:
                        dop = ops[d]
                        if dop.is_dma:
                            key = ("d", id(dop.dslot))
                            val = 16 * o.dma_wait_vals[d]
                            sem = dop.dslot.sem
                        else:
                            if dop.eng == ename and (ename == "pe" or d not in o.hard):
                                continue
                            key = ("e", dop.eng)
                            val = dop.milestone
                            sem = esem[dop.eng]
                        if seen.get(key, 0) >= val:
                            continue
                        if key not in w or w[key][1] < val:
                            w[key] = (sem, val)
                    for key, (sem, val) in w.items():
                        eobj.wait_ge(sem, val)
                        seen[key] = val
                    if o.fn is None:
                        if o.milestone is not None:
                            eobj.nop().then_inc(esem[ename], 1)
                        continue
                    if o.is_dma:
                        o.fn(eobj, o.dslot.sem)
                    else:
                        ins = o.fn(eobj)
                        if o.milestone is not None:
                            ins.then_inc(esem[ename], 1)

            @block.tensor
            def _(e):
                run_engine("pe", e)

            @block.scalar
            def _(e):
                run_engine("act", e)

            @block.vector
            def _(e):
                run_engine("dve", e)

            @block.gpsimd
            def _(e):
                run_engine("pool", e)

            @block.sync
            def _(e):
                run_engine("sp", e)
        return counters, nd


class Tl:
    _n = 0

    def __init__(self, t, name=None):
        Tl._n += 1
        self.t = t
        self.b = Buf(name or ("t%d" % Tl._n))


class Arena:
    def __init__(self, nc):
        self.nc = nc
        self.off = SB_BASE
        self.n = 0
        self.peak = 0

    def alloc(self, shape, dt, name=None):
        esz = 4 if dt == F32 else 2
        nbytes = int(np.prod(shape[1:])) * esz
        nbytes = (nbytes + 63) // 64 * 64
        assert self.off + nbytes <= SB_END, ("SBUF overflow", self.off, nbytes)
        self.n += 1
        t = self.nc.alloc_sbuf_tensor_at("sb%d" % self.n, list(shape), dt, offset=self.off)
        self.off += nbytes
        self.peak = max(self.peak, self.off)
        return Tl(t, name or ("sb%d" % self.n))

    def mark(self):
        return self.off

    def reset(self, m):
        self.off = m


def build(dbg=(), stop=None, depth=DEPTH):
    nc = bass.Bass("TRN2", target_bir_lowering=False)
    P = Prog(nc)
    ar = Arena(nc)
    es = contextlib.ExitStack()

    def din(name, shape):
        return nc.dram_tensor(name, list(shape), F32, kind="ExternalInput").ap()

    x_in = din("x", [T, D])
    cvec = din("cvec", [128, KC])
    normg = din("normg", [DEPTH, 3, 128, KC])
    w_ada = din("w_ada", [DEPTH, D, 9 * D])
    b_ada = din("b_ada", [DEPTH, 128, 144])
    fw1 = [din("ffn1_w1", [DEPTH, D, FF]), din("ffn2_w1", [DEPTH, D, FF])]
    fw3 = [din("ffn1_w3", [DEPTH, D, FF]), din("ffn2_w3", [DEPTH, D, FF])]
    fw2 = [din("ffn1_w2", [DEPTH, FF, D]), din("ffn2_w2", [DEPTH, FF, D])]
    w_in = din("w_in", [DEPTH, D, DIN])
    bif = din("bif", [DEPTH, 4, 2])
    convw = din("convw", [DEPTH, 128, 16, 4])
    convb = din("convb", [DEPTH, 128, 16])
    qkg = din("qkg", [DEPTH, 128, 2])
    relb = din("relb", [DEPTH, 128, HA, 5, 128])
    amask = din("amask", [128, 5, 128])
    mng = din("mng", [DEPTH, 128, 8])
    w_up_a = din("w_up_a", [DEPTH, 1024, D])
    w_up_m = din("w_up_m", [DEPTH, 1024, D])
    w_out = din("w_out", [DEPTH, D, D])
    out = nc.dram_tensor("out", [T, D], F32, kind="ExternalOutput").ap()

    def scratch(name, shape, dt):
        kind = "ExternalOutput" if name in dbg else "Internal"
        return nc.dram_tensor(name, list(shape), dt, kind=kind).ap()

    xT = scratch("xT", [KC, 128, T], F32)
    qaT = scratch("qaT", [HA, 128, T], BF16)
    kaT = scratch("kaT", [HA, 128, T], BF16)
    va = scratch("va", [T, 1024], BF16)
    qkmT = scratch("qkmT", [16, 128, T], BF16)
    vm = scratch("vm", [T, 1024], BF16)
    omT = scratch("omT", [8, 128, T], BF16)
    igT = scratch("igT", [4, T], F32)
    lfT = scratch("lfT", [4, T], F32)
    gaT = scratch("gaT", [KC, 128, T], BF16)
    gmT = scratch("gmT", [KC, 128, T], BF16)
    attnT = scratch("attnT", [HA, 128, T], BF16)
    hmT = scratch("hmT", [8, 128, T], BF16)
    rowsD = scratch("rowsD", [3, 4, T], F32)

    dbufs = {}

    def db(name, *idx):
        k = (name,) + idx
        if k not in dbufs:
            dbufs[k] = Buf("dr_" + "_".join(str(i) for i in k))
        return dbufs[k]

    PS = es.enter_context(nc.psum_tensor("PS", [128, 8, 512], F32))
    PSB = PS.bitcast(BF16) if hasattr(PS, "bitcast") else None
    pb = [Buf("psb%d" % i) for i in range(8)]

    def MM(o, lhsT, rhs, start, stop, reads, writes):
        P.op("pe", lambda e: e.matmul(o, lhsT=lhsT, rhs=rhs, start=start, stop=stop), reads, writes)

    def TR(o, in_, ident_ap, reads, writes):
        P.op("pe", lambda e: e.transpose(out=o, in_=in_, identity=ident_ap), reads, writes)

    def ACTF(o, in_, func, reads, writes, bias=None, scale=None):
        kw = {}
        if bias is not None:
            kw["bias"] = bias
        if scale is not None:
            kw["scale"] = scale
        P.op("act", lambda e: e.activation(out=o, in_=in_, func=func, **kw), reads, writes)

    def TT(eng, o, in0, in1, op, reads, writes):
        P.op(eng, lambda e: e.tensor_tensor(out=o, in0=in0, in1=in1, op=op), reads, writes)

    def TS(eng, o, in0, s1, s2, op0, op1, reads, writes):
        if s2 is None:
            P.op(eng, lambda e: e.tensor_scalar(out=o, in0=in0, scalar1=s1, scalar2=None, op0=op0), reads, writes)
        else:
            P.op(eng, lambda e: e.tensor_scalar(out=o, in0=in0, scalar1=s1, scalar2=s2, op0=op0, op1=op1), reads, writes)

    def STT(eng, o, in0, scalar, in1, op0, op1, reads, writes):
        P.op(eng, lambda e: e.scalar_tensor_tensor(out=o, in0=in0, scalar=scalar, in1=in1, op0=op0, op1=op1), reads, writes)

    def CP(eng, o, in_, reads, writes):
        if eng == "act":
            P.op("act", lambda e: e.copy(out=o, in_=in_), reads, writes)
        else:
            P.op(eng, lambda e: e.tensor_copy(out=o, in_=in_), reads, writes)

    def RECIP(o, in_, reads, writes):
        P.op("dve", lambda e: e.reciprocal(out=o, in_=in_), reads, writes)

    def MEMSET(eng, o, val, writes):
        P.op(eng, lambda e: e.memset(o, val), (), writes)

    def LD(eng, tl, o, in_, reads=(), nc_ok=False):
        if nc_ok:
            P.dma(eng, lambda e, s: e.dma_start(out=o, in_=in_, allow_slow_non_contiguous=True).then_inc(s, 16), tl.b,
                  reads=list(reads), writes=[tl.b])
        else:
            P.dma(eng, lambda e, s: e.dma_start(out=o, in_=in_).then_inc(s, 16), tl.b, reads=list(reads), writes=[tl.b])

    def ST(eng, tl, o, in_, writes=(), pwrites=()):
        P.dma(eng, lambda e, s: e.dma_start(out=o, in_=in_).then_inc(s, 16), tl.b, reads=[tl.b],
              writes=list(writes), pwrites=list(pwrites))

    ident = ar.alloc([128, 128], F32, "ident")
    identb = ar.alloc([128, 128], BF16, "identb")
    onesb = ar.alloc([128, 128], BF16, "onesb")
    tri = ar.alloc([128, 128], F32, "tri")
    sel = ar.alloc([4, 4, 128], F32, "sel")
    epsT = ar.alloc([128, 1], F32, "eps")
    cact = ar.alloc([128, KC], F32, "cact")
    cactb = ar.alloc([128, KC], BF16, "cactb")
    modv = ar.alloc([128, 144], F32, "modv")
    Amod = ar.alloc([128, 3, KC], F32, "Amod")
    gateh = ar.alloc([128, 3, KC], F32, "gateh")
    ngt = ar.alloc([128, 3, KC], F32, "ngt")
    bad = ar.alloc([128, 144], F32, "bad")
    cwT = ar.alloc([128, 16, 4], F32, "cw")
    cbT = ar.alloc([128, 16], F32, "cb")
    qkgT = ar.alloc([128, 2], F32, "qkg")
    mngT = ar.alloc([128, 8], F32, "mng")
    bifT = ar.alloc([4, 2], F32, "bif")
    nbfT = ar.alloc([4, 1], F32, "nbf")
    halo = ar.alloc([128, 16, 3], F32, "halo")

    MEMSET("pool", ident.t[:], 1.0, [ident.b])
    P.op("pool", lambda e: e.affine_select(out=ident.t[:], in_=ident.t[:], pattern=[[-1, 128]], compare_op=ALU.is_equal,
                                           fill=0.0, base=0, channel_multiplier=1), [ident.b], [ident.b])
    CP("pool", identb.t[:], ident.t[:], [ident.b], [identb.b])
    MEMSET("pool", onesb.t[:], 1.0, [onesb.b])
    MEMSET("pool", tri.t[:], 1.0, [tri.b])
    P.op("pool", lambda e: e.affine_select(out=tri.t[:], in_=tri.t[:], pattern=[[1, 128]], compare_op=ALU.is_ge,
                                           fill=0.0, base=0, channel_multiplier=-1), [tri.b], [tri.b])
    MEMSET("pool", sel.t[:], 1.0, [sel.b])
    P.op("pool", lambda e: e.affine_select(out=sel.t[:], in_=sel.t[:], pattern=[[1, 4], [0, 128]], compare_op=ALU.is_equal,
                                           fill=0.0, base=0, channel_multiplier=-1), [sel.b], [sel.b])
    MEMSET("pool", epsT.t[:], EPS, [epsT.b])
    LD("sp", cact, cact.t[:], cvec)
    ACTF(cact.t[:], cact.t[:], AF.Silu, [cact.b], [cact.b])
    CP("dve", cactb.t[:], cact.t[:], [cact.b], [cactb.b])
    persist_mark = ar.mark()

    def xbufs(tb, ks=range(KC)):
        return [db("xT", k, tb) for k in ks]

    def stage_in():
        m = ar.mark()
        xin = [ar.alloc([128, D], F32) for _ in range(2)]
        xo = [ar.alloc([128, KC, 128], F32) for _ in range(2)]
        for tt in range(T // 128):
            a = xin[tt % 2]
            o = xo[tt % 2]
            LD("sp", a, a.t[:], x_in[tt * 128:(tt + 1) * 128, :])
            for q in range(4):
                bank = (tt * 4 + q) % 8
                for i in range(4):
                    k = q * 4 + i
                    TR(PS[:, bank, i * 128:(i + 1) * 128], a.t[:, k * 128:(k + 1) * 128], ident.t[:],
                       [a.b, ident.b], [pb[bank]])
                ceng = "act" if q % 2 else "dve"
                cin = PS[:, bank, :].rearrange("p (i t) -> p i t", i=4)
                cout = o.t[:, q * 4:(q + 1) * 4, :]
                P.op(ceng, (lambda e, cout=cout, cin=cin: e.copy(out=cout, in_=cin)) if ceng == "act" else
                     (lambda e, cout=cout, cin=cin: e.tensor_copy(out=cout, in_=cin)), [pb[bank]],
                     [o.b] if q == 0 else [], [] if q == 0 else [o.b])
            ST("sp", o, xT[:, :, tt * 128:(tt + 1) * 128].rearrange("k p t -> p k t"), o.t[:],
               pwrites=xbufs(tt // 4))
        P.barrier([t.b for t in xin + xo] + pb)
        ar.reset(m)

    def stage_mod(l):
        m = ar.mark()
        wt = [ar.alloc([128, KC, 512], BF16) for _ in range(2)]
        LD("sp", bad, bad.t[:], b_ada[l])
        LD("sp", ngt, ngt.t[:], normg[l].rearrange("i p k -> p i k"))
        LD("sp", cwT, cwT.t[:], convw[l])
        LD("sp", cbT, cbT.t[:], convb[l])
        LD("sp", qkgT, qkgT.t[:], qkg[l])
        LD("sp", mngT, mngT.t[:], mng[l])
        LD("sp", bifT, bifT.t[:], bif[l])
        TS("dve", qkgT.t[:, 0:1], qkgT.t[:, 0:1], float(128 ** -0.5), None, ALU.mult, None, [qkgT.b], [qkgT.b])
        TS("dve", nbfT.t[:], bifT.t[:, 1:2], -1.0, None, ALU.mult, None, [bifT.b], [nbfT.b])
        for blk in range(36):
            w = wt[blk % 2]
            LD("pool", w, w.t[:], w_ada[l, :, blk * 512:(blk + 1) * 512].rearrange("(k p) n -> p k n", p=128))
            for c4 in range(4):
                j = blk * 4 + c4
                for k in range(KC):
                    MM(PS[:, 0, j:j + 1], w.t[:, k, c4 * 128:(c4 + 1) * 128], cactb.t[:, k:k + 1], k == 0, k == KC - 1,
                       [w.b, cactb.b], [pb[0]])
        TT("dve", modv.t[:], PS[:, 0, 0:144], bad.t[:], ALU.add, [pb[0], bad.b], [modv.b])
        mv = modv.t[:].rearrange("p (j k) -> p j k", j=9)
        for i in range(3):
            STT("dve", Amod.t[:, i, :], mv[:, 3 * i + 1, :], 1.0, ngt.t[:, i, :], ALU.add, ALU.mult,
                [modv.b, ngt.b], [Amod.b])
            TS("dve", gateh.t[:, i, :], mv[:, 3 * i + 2, :], 0.5 if i != 1 else 1.0, None, ALU.mult, None,
               [modv.b], [gateh.b])
        P.barrier([t.b for t in wt] + [pb[0]])
        ar.reset(m)

    def shift_ap(i, k):
        return modv.t[:, (3 * i) * KC + k:(3 * i) * KC + k + 1]

    def norm_block(i, tb, xm, col0, xin, sq, rs, tmp, bank):
        LD("sp", xin, xin.t[:], xT[:, :, tb * 512:(tb + 1) * 512].rearrange("k p t -> p k t"), reads=xbufs(tb))
        for g4 in range(4):
            s = sq[g4 % 2]
            ACTF(s.t[:], xin.t[:, g4 * 4:(g4 + 1) * 4, :], AF.Square, [xin.b], [s.b])
            for kk in range(4):
                k = g4 * 4 + kk
                MM(PS[:, bank, :], onesb.t[:], s.t[:, kk, :], k == 0, k == KC - 1, [onesb.b, s.b], [pb[bank]])
        ACTF(rs.t[:], PS[:, bank, :], AF.Sqrt, [pb[bank], epsT.b], [rs.b], bias=epsT.t[:, 0:1], scale=1.0 / D)
        RECIP(rs.t[:], rs.t[:], [rs.b], [rs.b])
        for k in range(KC):
            t_ = tmp[k % 2]
            STT("dve", t_.t[:], xin.t[:, k, :], Amod.t[:, i, k:k + 1], rs.t[:], ALU.mult, ALU.mult,
                [xin.b, Amod.b, rs.b], [t_.b])
            ACTF(xm.t[:, k, col0:col0 + 512], t_.t[:], AF.Identity, [t_.b, modv.b], [xm.b], bias=shift_ap(i, k))

    def resid_epilogue(i, d, tb, bank, xold, xnew):
        LD("sp", xold, xold.t[:], xT[d, :, tb * 512:(tb + 1) * 512], reads=[db("xT", d, tb)])
        STT("dve", xnew.t[:], PS[:, bank, :], gateh.t[:, i, d:d + 1], xold.t[:], ALU.mult, ALU.add,
            [pb[bank], gateh.b, xold.b], [xnew.b])
        ST("sp", xnew, xT[d, :, tb * 512:(tb + 1) * 512], xnew.t[:], writes=[db("xT", d, tb)])

    def stage_ffn(l, i):
        fi = 0 if i == 0 else 1
        w1d, w3d, w2d = fw1[fi], fw3[fi], fw2[fi]
        m = ar.mark()
        xm = ar.alloc([128, KC, 1024], BF16, "xm")
        g = ar.alloc([128, FC, 1024], BF16, "g")
        xp = [ar.alloc([128, KC, 128], F32) for _ in range(2)]
        sqp = [ar.alloc([128, KC, 128], BF16) for _ in range(2)]
        rp = [ar.alloc([128, 128], F32) for _ in range(2)]
        w13 = [[ar.alloc([128, KC, 128], BF16) for _ in range(3)] for _ in range(2)]
        w2 = [ar.alloc([128, FC, 128], BF16) for _ in range(2)]
        sil = [ar.alloc([128, 512], F32) for _ in range(2)]
        xold = [ar.alloc([128, 512], F32) for _ in range(2)]
        xnew = [ar.alloc([128, 512], F32) for _ in range(2)]

        def piece_front(q):
            tb2, pc = divmod(q, 8)
            t0 = tb2 * 1024 + pc * 128
            x_, s_ = xp[q % 2], sqp[q % 2]
            LD("sp", x_, x_.t[:], xT[:, :, t0:t0 + 128].rearrange("k p t -> p k t"), reads=xbufs(t0 // 512))
            ACTF(s_.t[:], x_.t[:], AF.Square, [x_.b], [s_.b])

        def piece_back(q):
            tb2, pc = divmod(q, 8)
            x_, s_, r_ = xp[q % 2], sqp[q % 2], rp[q % 2]
            c0 = (q % 4) * 128
            for k in range(KC):
                MM(PS[:, 0, c0:c0 + 128], onesb.t[:], s_.t[:, k, :], k == 0, k == KC - 1, [onesb.b, s_.b], [pb[0]])
            ACTF(r_.t[:], PS[:, 0, c0:c0 + 128], AF.Sqrt, [pb[0], epsT.b], [r_.b], bias=epsT.t[:, 0:1], scale=1.0 / D)
            RECIP(r_.t[:], r_.t[:], [r_.b], [r_.b])
            TT("dve", x_.t[:], x_.t[:], r_.t[:, None, :].to_broadcast([128, KC, 128]), ALU.mult, [x_.b, r_.b], [x_.b])
            for k in range(KC):
                ACTF(xm.t[:, k, pc * 128:(pc + 1) * 128], x_.t[:, k, :], AF.Identity, [x_.b, Amod.b, modv.b], [xm.b],
                     bias=shift_ap(i, k), scale=Amod.t[:, i, k:k + 1])

        cnt = 0
        nblk = T // 1024
        piece_front(0)
        for pc in range(8):
            if pc + 1 < 8:
                piece_front(pc + 1)
            piece_back(pc)
        for tb2 in range(nblk):
            for f in range(FC):
                wa = w13[0][f % 3]
                wb = w13[1][f % 3]
                LD("pool", wa, wa.t[:], w1d[l, :, f * 128:(f + 1) * 128].rearrange("(k p) n -> p k n", p=128))
                LD("pool", wb, wb.t[:], w3d[l, :, f * 128:(f + 1) * 128].rearrange("(k p) n -> p k n", p=128))
                for h in range(2):
                    b1 = (cnt * 2) % 6
                    b3 = b1 + 1
                    for k in range(KC):
                        MM(PS[:, b1, :], wa.t[:, k, :], xm.t[:, k, h * 512:(h + 1) * 512], k == 0, k == KC - 1,
                           [wa.b, xm.b], [pb[b1]])
                    for k in range(KC):
                        MM(PS[:, b3, :], wb.t[:, k, :], xm.t[:, k, h * 512:(h + 1) * 512], k == 0, k == KC - 1,
                           [wb.b, xm.b], [pb[b3]])
                    s_ = sil[cnt % 2]
                    ACTF(s_.t[:], PS[:, b1, :], AF.Silu, [pb[b1]], [s_.b])
                    TT("dve", g.t[:, f, h * 512:(h + 1) * 512], s_.t[:], PS[:, b3, :], ALU.mult,
                       [s_.b, pb[b3]], [g.b])
                    cnt += 1
            nxt = tb2 + 1 < nblk
            for d in range(KC):
                if nxt and d % 2 == 0:
                    piece_front((tb2 + 1) * 8 + d // 2)
                w = w2[d % 2]
                LD("pool", w, w.t[:], w2d[l, :, d * 128:(d + 1) * 128].rearrange("(f p) n -> p f n", p=128))
                for h in range(2):
                    bank = 6 + (cnt % 2)
                    for f in range(FC):
                        MM(PS[:, bank, :], w.t[:, f, :], g.t[:, f, h * 512:(h + 1) * 512], f == 0, f == FC - 1,
                           [w.b, g.b], [pb[bank]])
                    resid_epilogue(i, d, tb2 * 2 + h, bank, xold[cnt % 2], xnew[cnt % 2])
                    cnt += 1
                if nxt and d % 2 == 1:
                    piece_back((tb2 + 1) * 8 + d // 2)
        allb = [xm.b, g.b] + [t.b for t in xp + sqp + rp + w13[0] + w13[1] + w2 + sil + xold + xnew] + pb
        P.barrier(allb)
        ar.reset(m)

    def stage_proj(l):
        m = ar.mark()
        xm = ar.alloc([128, KC, 1024], BF16, "xm")
        xin = ar.alloc([128, KC, 512], F32)
        sq = [ar.alloc([128, 4, 512], BF16) for _ in range(2)]
        rs = ar.alloc([128, 512], F32)
        tmp = [ar.alloc([128, 512], F32) for _ in range(2)]
        wt = [ar.alloc([128, KC, 128], BF16) for _ in range(3)]
        wv = [ar.alloc([128, KC, 512], BF16) for _ in range(2)]
        wif = ar.alloc([128, KC, 8], BF16)
        osb = [ar.alloc([128, 512], BF16) for _ in range(4)]
        sqb = [ar.alloc([128, 512], BF16) for _ in range(2)]
        rs2 = [ar.alloc([128, 512], F32) for _ in range(2)]
        uext = [ar.alloc([128, 515], F32) for _ in range(2)]
        acc = [ar.alloc([128, 512], F32) for _ in range(2)]
        rows = [ar.alloc([4, 512], F32) for _ in range(4)]
        MEMSET("dve", halo.t[:], 0.0, [halo.b])
        LD("pool", wif, wif.t[:], w_in[l, :, 7168:7176].rearrange("(k p) n -> p k n", p=128), nc_ok=True)
        cnt = [0]
        pend = []

        def flush():
            while pend:
                pend.pop(0)()

        def fm_group(c0, tb2, epi):
            w = wt[cnt[0] % 3]
            LD("pool", w, w.t[:], w_in[l, :, c0:c0 + 128].rearrange("(k p) n -> p k n", p=128))
            for h in range(2):
                bank = cnt[0] % 4
                cnt[0] += 1
                for k in range(KC):
                    MM(PS[:, bank, :], w.t[:, k, :], xm.t[:, k, h * 512:(h + 1) * 512], k == 0, k == KC - 1,
                       [w.b, xm.b], [pb[bank]])
                flush()
                epi(bank, tb2 * 2 + h, h)

        def epi_qk(which, head):
            dst = qaT if which == 0 else kaT

            def epi(bank, tb, h):
                n = cnt[0]
                s = sqb[n % 2]
                r = rs2[n % 2]
                o = osb[n % 4]
                b2 = 4 + n % 2
                ACTF(s.t[:], PS[:, bank, :], AF.Square, [pb[bank]], [s.b])

                def later():
                    MM(PS[:, b2, :], onesb.t[:], s.t[:], True, True, [onesb.b, s.b], [pb[b2]])
                    ACTF(r.t[:], PS[:, b2, :], AF.Sqrt, [pb[b2], epsT.b], [r.b], bias=epsT.t[:, 0:1], scale=1.0 / 128)
                    RECIP(r.t[:], r.t[:], [r.b], [r.b])
                    STT("dve", o.t[:], PS[:, bank, :], qkgT.t[:, which:which + 1], r.t[:], ALU.mult, ALU.mult,
                        [pb[bank], qkgT.b, r.b], [o.b])
                    ST("sp", o, dst[head, :, tb * 512:(tb + 1) * 512], o.t[:], pwrites=[db("qk", tb)])
                pend.append(later)
            return epi

        def epi_sig(dst, ch, name):
            def epi(bank, tb, h):
                o = osb[cnt[0] % 4]
                ACTF(o.t[:], PS[:, bank, :], AF.Sigmoid, [pb[bank]], [o.b])
                ST("sp", o, dst[ch, :, tb * 512:(tb + 1) * 512], o.t[:], pwrites=[db(name, tb)])
            return epi

        def epi_conv(ch):
            def epi(bank, tb, h):
                n = cnt[0]
                u = uext[n % 2]
                a = acc[n % 2]
                o = osb[n % 4]
                CP("act", u.t[:, 3:515], PS[:, bank, :], [pb[bank]], [u.b])
                CP("act", u.t[:, 0:3], halo.t[:, ch, :], [halo.b], [u.b])
                CP("act", halo.t[:, ch, :], u.t[:, 512:515], [u.b], [halo.b])
                TS("dve", a.t[:], u.t[:, 0:512], cwT.t[:, ch, 0:1], None, ALU.mult, None, [u.b, cwT.b], [a.b])
                for j in range(1, 4):
                    STT("dve", a.t[:], u.t[:, j:j + 512], cwT.t[:, ch, j:j + 1], a.t[:], ALU.mult, ALU.add,
                        [u.b, cwT.b, a.b], [a.b])
                ACTF(o.t[:], a.t[:], AF.Silu, [a.b, cbT.b], [o.b], bias=cbT.t[:, ch:ch + 1])
                ST("sp", o, qkmT[ch, :, tb * 512:(tb + 1) * 512], o.t[:], pwrites=[db("qkm", tb)])
            return epi

        def tm_block(c0, dst, tb2, name):
            w = wv[cnt[0] % 2]
            LD("pool", w, w.t[:], w_in[l, :, c0:c0 + 512].rearrange("(k p) n -> p k n", p=128))
            cc = (c0 % 1024)
            for tt in range(8):
                bank = cnt[0] % 4
                o = osb[cnt[0] % 4]
                cnt[0] += 1
                for k in range(KC):
                    MM(PS[:, bank, :], xm.t[:, k, tt * 128:(tt + 1) * 128], w.t[:, k, :], k == 0, k == KC - 1,
                       [w.b, xm.b], [pb[bank]])
                flush()
                CP("act" if tt % 2 else "dve", o.t[:], PS[:, bank, :], [pb[bank]], [o.b])
                t0 = tb2 * 1024 + tt * 128
                ST("sp", o, dst[t0:t0 + 128, cc:cc + 512], o.t[:], pwrites=[db(name, t0 // 512)])

        for tb2 in range(T // 1024):
            for h in range(2):
                norm_block(1, tb2 * 2 + h, xm, h * 512, xin, sq, rs, tmp, 7)
            for hd in range(HA):
                fm_group(hd * 128, tb2, epi_qk(0, hd))
            for hd in range(HA):
                fm_group(1024 + hd * 128, tb2, epi_qk(1, hd))
            for cb in range(2):
                tm_block(2048 + cb * 512, va, tb2, "va")
            for ch in range(16):
                fm_group(3072 + ch * 128, tb2, epi_conv(ch))
            for cb in range(2):
                tm_block(5120 + cb * 512, vm, tb2, "vm")
            for ch in range(8):
                fm_group(6144 + ch * 128, tb2, epi_sig(omT, ch, "om"))
            for h in range(2):
                tb = tb2 * 2 + h
                for gi in range(2):
                    bank = cnt[0] % 4
                    cnt[0] += 1
                    r = rows[(h * 2 + gi) % 4]
                    for k in range(KC):
                        MM(PS[0:4, bank, :], wif.t[:, k, gi * 4:(gi + 1) * 4], xm.t[:, k, h * 512:(h + 1) * 512],
                           k == 0, k == KC - 1, [wif.b, xm.b], [pb[bank]])
                    flush()
                    if gi == 0:
                        ACTF(r.t[:], PS[0:4, bank, :], AF.Identity, [pb[bank], bifT.b], [r.b], bias=bifT.t[:, 0:1])
                        ST("sp", r, igT[:, tb * 512:(tb + 1) * 512], r.t[:], pwrites=[db("gates", tb)])
                    else:
                        ACTF(r.t[:], PS[0:4, bank, :], AF.Exp, [pb[bank], nbfT.b], [r.b], bias=nbfT.t[:, 0:1], scale=-1.0)
                        ACTF(r.t[:], r.t[:], AF.Ln, [r.b], [r.b], bias=1.0)
                        TS("dve", r.t[:], r.t[:], -1.0, None, ALU.mult, None, [r.b], [r.b])
                        ST("sp", r, lfT[:, tb * 512:(tb + 1) * 512], r.t[:], pwrites=[db("gates", tb)])
            for ch in range(KC):
                fm_group(7176 + ch * 128, tb2, epi_sig(gaT, ch, "ga"))
            for ch in range(KC):
                fm_group(9224 + ch * 128, tb2, epi_sig(gmT, ch, "gm"))
            flush()
        allb = [xm.b, xin.b, rs.b, wif.b, halo.b] + [t.b for t in sq + tmp + wt + wv + osb + sqb + rs2 + uext + acc + rows] + pb
        P.barrier(allb)
        ar.reset(m)

    def stage_attn(l):
        m = ar.mark()
        bias = ar.alloc([128, HA, 5, 128], F32, "bias")
        msk = ar.alloc([128, 5, 128], F32)
        kwin = [ar.alloc([128, HA, 1024], BF16) for _ in range(2)]
        vwin = [ar.alloc([128, 8, 1024], BF16) for _ in range(2)]
        qblk = [ar.alloc([128, HA, 512], BF16) for _ in range(2)]
        oblk = [ar.alloc([128, HA, 512], BF16) for _ in range(2)]
        sb_ = [ar.alloc([128, 5, 128], F32) for _ in range(2)]
        pT = [ar.alloc([128, 5, 128], BF16) for _ in range(2)]
        rden = [ar.alloc([128, 128], F32) for _ in range(2)]
        LD("sp", bias, bias.t[:], relb[l])
        LD("sp", msk, msk.t[:], amask)
        for hd in range(HA):
            TT("dve", bias.t[:, hd], bias.t[:, hd], msk.t[:], ALU.add, [bias.b, msk.b], [bias.b])
        PS2 = PS[:].rearrange("p (a b) n -> p a (b n)", b=2)
        n = 0
        prev = None
        for tb in range(NB):
            kw, vw, qb, ob = kwin[tb % 2], vwin[tb % 2], qblk[tb % 2], oblk[tb % 2]
            lo = 0 if tb > 0 else 512
            t_lo = tb * 512 - 512 + lo
            rd = [db("qk", tb), db("va", tb)] + ([db("qk", tb - 1), db("va", tb - 1)] if tb > 0 else [])
            LD("sp", kw, kw.t[:, :, lo:1024], kaT[:, :, t_lo:(tb + 1) * 512].rearrange("h p t -> p h t"), reads=rd)
            LD("sp", vw, vw.t[:, lo // 128:8, :], va[t_lo:(tb + 1) * 512, :].rearrange("(tt p) d -> p tt d", p=128), reads=rd)
            LD("sp", qb, qb.t[:], qaT[:, :, tb * 512:(tb + 1) * 512].rearrange("h p t -> p h t"), reads=rd)
            for jj in range(4):
                j = tb * 4 + jj
                kt0 = max(0, 4 - j)
                for hd in range(HA):
                    sreg = n % 2
                    sbufs = [pb[2 * sreg], pb[2 * sreg + 1]]
                    for kt in range(kt0, 5):
                        MM(PS2[:, sreg, kt * 128:(kt + 1) * 128], kw.t[:, hd, (jj + kt) * 128:(jj + kt + 1) * 128],
                           qb.t[:, hd, jj * 128:(jj + 1) * 128], True, True, [kw.b, qb.b], sbufs)
                    s_ = sb_[n % 2]
                    p_ = pT[n % 2]
                    TT("dve", s_.t[:, kt0:5, :], PS2[:, sreg, kt0 * 128:640].rearrange("p (a b) -> p a b", b=128),
                       bias.t[:, hd, kt0:5, :], ALU.add, sbufs + [bias.b], [s_.b])
                    ACTF(p_.t[:, kt0:5, :], s_.t[:, kt0:5, :], AF.Exp, [s_.b], [p_.b])
                    if prev is not None:
                        prev()

                    def pv(p_=p_, vw=vw, ob=ob, hd=hd, jj=jj, kt0=kt0, n=n):
                        obank = 4 + (n % 4)
                        r_ = rden[n % 2]
                        for kt in range(kt0, 5):
                            MM(PS[:, obank, 0:128], vw.t[:, jj + kt, hd * 128:(hd + 1) * 128], p_.t[:, kt, :],
                               kt == kt0, kt == 4, [vw.b, p_.b], [pb[obank]])
                        for kt in range(kt0, 5):
                            MM(PS[:, obank, 128:256], onesb.t[:], p_.t[:, kt, :], kt == kt0, kt == 4,
                               [onesb.b, p_.b], [pb[obank]])
                        RECIP(r_.t[:], PS[:, obank, 128:256], [pb[obank]], [r_.b])
                        TT("dve", ob.t[:, hd, jj * 128:(jj + 1) * 128], PS[:, obank, 0:128], r_.t[:], ALU.mult,
                           [pb[obank], r_.b], [ob.b])
                    prev = pv
                    n += 1
            prev()
            prev = None
            ST("sp", ob, attnT[:, :, tb * 512:(tb + 1) * 512].rearrange("h p t -> p h t"), ob.t[:], writes=[db("attn", tb)])
        allb = [bias.b, msk.b] + [t.b for t in kwin + vwin + qblk + oblk + sb_ + pT + rden] + pb
        P.barrier(allb)
        ar.reset(m)

    LM = 128
    LN16 = float(np.log(1.0 / 16.0))

    def stage_mlstm(l):
        m = ar.mark()
        Cst = ar.alloc([128, HM, 2, 384], F32, "Cst")
        Cbf = ar.alloc([128, HM, 2, 384], BF16, "Cbf")
        qk = [[ar.alloc([128, 4, 512], BF16) for _ in range(2)] for _ in range(HM)]
        vaug = [[ar.alloc([128, 4, 384], BF16) for _ in range(2)] for _ in range(HM)]
        qh = [[ar.alloc([128, 2, 512], BF16) for _ in range(2)] for _ in range(HM)]
        kh = [[ar.alloc([128, 2, 512], BF16) for _ in range(2)] for _ in range(HM)]
        kw_ = [ar.alloc([128, 2, 512], BF16) for _ in range(2)]
        ktok = [[ar.alloc([128, 4, 256], BF16) for _ in range(2)] for _ in range(HM)]
        ebl = [[ar.alloc([128, 4], F32) for _ in range(2)] for _ in range(HM)]
        wT = [ar.alloc([128, 128], BF16) for _ in range(4)]
        hT = [[ar.alloc([128, 2, 512], F32) for _ in range(2)] for _ in range(HM)]
        aden = [ar.alloc([128, 128], F32) for _ in range(2)]
        omb = [ar.alloc([128, 2, 512], BF16) for _ in range(2)]
        sqh = [ar.alloc([128, 2, 512], BF16) for _ in range(2)]
        rsh = [ar.alloc([128, 512], F32) for _ in range(2)]
        hn = [ar.alloc([128, 2, 512], F32) for _ in range(2)]
        hob = [ar.alloc([128, 2, 512], BF16) for _ in range(2)]
        gd = [db("gates", tb) for tb in range(NB)]
        glf = ar.alloc([128, LM], F32, "glf")
        glf2 = ar.alloc([128, LM], F32, "glf2")
        gig = ar.alloc([128, LM], F32, "gig")
        geb = ar.alloc([128, LM], F32, "geb")
        geu = ar.alloc([128, LM], F32, "geu")
        gew = ar.alloc([128, LM], F32, "gew")
        rowt = [ar.alloc([4, 3, 512], F32) for _ in range(2)]
        LD("sp", glf, glf.t[:], lfT.rearrange("h (c t) -> (h c) t", t=LM), reads=gd)
        LD("sp", gig, gig.t[:], igT.rearrange("h (c t) -> (h c) t", t=LM), reads=gd)
        src, dst = glf, glf2
        s = 1
        while s < LM:
            TT("dve", dst.t[:, s:LM], src.t[:, s:LM], src.t[:, 0:LM - s], ALU.add, [src.b], [dst.b])
            CP("dve", dst.t[:, 0:s], src.t[:, 0:s], [src.b], [dst.b])
            src, dst = dst, src
            s *= 2
        bcs, oth = src, dst
        ACTF(geb.t[:], bcs.t[:], AF.Exp, [bcs.b], [geb.b])
        STT("dve", oth.t[:], gig.t[:], LN16, bcs.t[:], ALU.add, ALU.subtract, [gig.b, bcs.b], [oth.b])
        ACTF(geu.t[:], oth.t[:], AF.Exp, [oth.b], [geu.b])
        TS("dve", oth.t[:], oth.t[:], bcs.t[:, LM - 1:LM], None, ALU.add, None, [oth.b, bcs.b], [oth.b])
        ACTF(gew.t[:], oth.t[:], AF.Exp, [oth.b], [gew.b])
        for ri, gt in enumerate((geb, geu, gew)):
            ST("sp", gt, rowsD[ri].rearrange("h (c t) -> (h c) t", t=LM), gt.t[:], writes=[db("rows", ri)])
        MEMSET("dve", Cst.t[:], 0.0, [Cst.b])
        MEMSET("dve", Cbf.t[:], 0.0, [Cbf.b])
        for hd in range(HM):
            for s2 in range(2):
                MEMSET("pool", vaug[hd][s2].t[:, :, 256:384], 1.0, [vaug[hd][s2].b])
        n = [0]
        for tb in range(NB):
            sl = tb % 2
            tok = slice(tb * 512, (tb + 1) * 512)
            rw = rowt[sl]
            LD("sp", rw, rw.t[:], rowsD[:, :, tok].rearrange("r h t -> h r t"), reads=[db("rows", 0), db("rows", 1), db("rows", 2)])
            for hd in range(HM):
                q_ = qk[hd][sl]
                v_ = vaug[hd][sl]
                LD("sp", q_, q_.t[:, 0:2, :], qkmT[2 * hd:2 * hd + 2, :, tok].rearrange("c p t -> p c t"), reads=[db("qkm", tb)])
                LD("sp", q_, q_.t[:, 2:4, :], qkmT[8 + 2 * hd:8 + 2 * hd + 2, :, tok].rearrange("c p t -> p c t"), reads=[db("qkm", tb)])
                LD("sp", v_, v_.t[:, :, 0:256], vm[tok, hd * 256:(hd + 1) * 256].rearrange("(tt p) d -> p tt d", p=128),
                   reads=[db("vm", tb)])
                qh_, kh_, kw2, kt_, eb_ = qh[hd][sl], kh[hd][sl], kw_[hd % 2], ktok[hd][sl], ebl[hd][sl]
                MM(PS[:, 0, :], sel.t[:, hd, :], rw.t[:, 0, :], True, True, [sel.b, rw.b], [pb[0]])
                TT("dve", qh_.t[:], q_.t[:, 0:2, :], PS[:, 0:1, :].to_broadcast([128, 2, 512]), ALU.mult, [q_.b, pb[0]], [qh_.b])
                CP("act", eb_.t[:], PS[:, 0, :].rearrange("p (c t) -> p c t", t=LM)[:, :, LM - 1], [pb[0]], [eb_.b])
                MM(PS[:, 1, :], sel.t[:, hd, :], rw.t[:, 1, :], True, True, [sel.b, rw.b], [pb[1]])
                TT("dve", kh_.t[:], q_.t[:, 2:4, :], PS[:, 1:2, :].to_broadcast([128, 2, 512]), ALU.mult, [q_.b, pb[1]], [kh_.b])
                MM(PS[:, 0, :], sel.t[:, hd, :], rw.t[:, 2, :], True, True, [sel.b, rw.b], [pb[0]])
                TT("dve", kw2.t[:], q_.t[:, 2:4, :], PS[:, 0:1, :].to_broadcast([128, 2, 512]), ALU.mult, [q_.b, pb[0]], [kw2.b])
                for c in range(4):
                    for dk in range(2):
                        TR(PSB[:, 2, c * 256 + dk * 128:c * 256 + (dk + 1) * 128], kw2.t[:, dk, c * 128:(c + 1) * 128],
                           identb.t[:], [kw2.b, identb.b], [pb[2]])
                CP("act", kt_.t[:], PSB[:, 2, 0:1024].rearrange("p (c d) -> p c d", c=4), [pb[2]], [kt_.b])
            for c in range(4):
                cs = slice(c * LM, (c + 1) * LM)
                for hd in range(HM):
                    q_, v_, qh_, kh_, kt_, eb_, h_ = qk[hd][sl], vaug[hd][sl], qh[hd][sl], kh[hd][sl], ktok[hd][sl], ebl[hd][sl], hT[hd][sl]
                    u = n[0]
                    n[0] += 1
                    w_ = wT[u % 4]
                    sslot = u % 4
                    for dk in range(2):
                        MM(PS[:, 3, sslot * 128:(sslot + 1) * 128], kh_.t[:, dk, cs], qh_.t[:, dk, cs], dk == 0, dk == 1,
                           [kh_.b, qh_.b], [pb[3]])
                    TT("dve", w_.t[:], PS[:, 3, sslot * 128:(sslot + 1) * 128], tri.t[:], ALU.mult, [pb[3], tri.b], [w_.b])
                    ob = 4 + u % 2
                    for dv in range(2):
                        MM(PS[:, ob, dv * 128:(dv + 1) * 128], v_.t[:, c, dv * 128:(dv + 1) * 128], w_.t[:], True, False,
                           [v_.b, w_.b], [pb[ob]])
                        for dk in range(2):
                            MM(PS[:, ob, dv * 128:(dv + 1) * 128], Cbf.t[:, hd, dk, dv * 128:(dv + 1) * 128], qh_.t[:, dk, cs],
                               False, dk == 1, [Cbf.b, qh_.b], [pb[ob]])
                    MM(PS[:, ob, 256:384], v_.t[:, c, 256:384], w_.t[:], True, False, [v_.b, w_.b], [pb[ob]])
                    for dk in range(2):
                        MM(PS[:, ob, 256:384], Cbf.t[:, hd, dk, 256:384], qh_.t[:, dk, cs], False, dk == 1,
                           [Cbf.b, qh_.b], [pb[ob]])
                    for dk in range(2):
                        MM(PS[:, 6 + dk, 0:384], kt_.t[:, c, dk * 128:(dk + 1) * 128], v_.t[:, c, :], True, True,
                           [kt_.b, v_.b], [pb[6 + dk]])
                    for dk in range(2):
                        STT("dve", Cst.t[:, hd, dk, :], Cst.t[:, hd, dk, :], eb_.t[:, c:c + 1], PS[:, 6 + dk, 0:384],
                            ALU.mult, ALU.add, [Cst.b, eb_.b, pb[6 + dk]], [Cst.b])
                    a_ = aden[u % 2]
                    ACTF(a_.t[:], PS[:, ob, 256:384], AF.Abs, [pb[ob]], [a_.b])
                    TS("dve", a_.t[:], a_.t[:], 1.0, None, ALU.max, None, [a_.b], [a_.b])
                    RECIP(a_.t[:], a_.t[:], [a_.b], [a_.b])
                    TT("dve", h_.t[:, :, cs], PS[:, ob, 0:256].rearrange("p (a b) -> p a b", a=2),
                       a_.t[:, None, :].to_broadcast([128, 2, 128]), ALU.mult, [pb[ob], a_.b], [h_.b])
                    CP("act", Cbf.t[:, hd], Cst.t[:, hd], [Cst.b], [Cbf.b])
            for hd in range(HM):
                h_ = hT[hd][sl]
                u = n[0]
                n[0] += 1
                o_, s_, r_, hn_, ho_ = omb[u % 2], sqh[u % 2], rsh[u % 2], hn[u % 2], hob[u % 2]
                LD("sp", o_, o_.t[:], omT[2 * hd:2 * hd + 2, :, tok].rearrange("c p t -> p c t"), reads=[db("om", tb)])
                ACTF(s_.t[:], h_.t[:], AF.Square, [h_.b], [s_.b])
                bnk = u % 2
                for dv in range(2):
                    MM(PS[:, bnk, :], onesb.t[:], s_.t[:, dv, :], dv == 0, dv == 1, [onesb.b, s_.b], [pb[bnk]])
                ACTF(r_.t[:], PS[:, bnk, :], AF.Sqrt, [pb[bnk], epsT.b], [r_.b], bias=epsT.t[:, 0:1], scale=1.0 / 256)
                RECIP(r_.t[:], r_.t[:], [r_.b], [r_.b])
                for dv in range(2):
                    STT("dve", hn_.t[:, dv, :], h_.t[:, dv, :], mngT.t[:, 2 * hd + dv:2 * hd + dv + 1], r_.t[:],
                        ALU.mult, ALU.mult, [h_.b, mngT.b, r_.b], [hn_.b])
                TT("dve", ho_.t[:], hn_.t[:], o_.t[:], ALU.mult, [hn_.b, o_.b], [ho_.b])
                ST("sp", ho_, hmT[2 * hd:2 * hd + 2, :, tok].rearrange("c p t -> p c t"), ho_.t[:], pwrites=[db("hm", tb)])
        allb = [glf.b, glf2.b, gig.b, geb.b, geu.b, gew.b, Cst.b, Cbf.b] + [t.b for t in rowt]
        for grp in (qk, vaug, qh, kh, ktok, ebl, hT):
            for pr in grp:
                allb += [t.b for t in pr]
        allb += [t.b for t in kw_ + wT + aden + omb + sqh + rsh + hn + hob] + pb
        P.barrier(allb)
        ar.reset(m)

    def stage_merge(l):
        m = ar.mark()
        at = ar.alloc([128, 8, 1024], BF16, "at")
        hm_ = ar.alloc([128, 8, 1024], BF16, "hm")
        mg = ar.alloc([128, KC, 1024], BF16, "mg")
        wa = [ar.alloc([128, 8, 128], BF16) for _ in range(2)]
        wm = [ar.alloc([128, 8, 128], BF16) for _ in range(2)]
        wo = [ar.alloc([128, KC, 128], BF16) for _ in range(2)]
        ga = [ar.alloc([128, 512], BF16) for _ in range(2)]
        gm = [ar.alloc([128, 512], BF16) for _ in range(2)]
        t1 = [ar.alloc([128, 512], F32) for _ in range(2)]
        t2 = [ar.alloc([128, 512], F32) for _ in range(2)]
        xold = [ar.alloc([128, 512], F32) for _ in range(2)]
        xnew = [ar.alloc([128, 512], F32) for _ in range(2)]
        cnt = 0
        for tb2 in range(T // 1024):
            tk = slice(tb2 * 1024, (tb2 + 1) * 1024)
            LD("sp", at, at.t[:], attnT[:, :, tk].rearrange("h p t -> p h t"), reads=[db("attn", 2 * tb2), db("attn", 2 * tb2 + 1)])
            LD("sp", hm_, hm_.t[:], hmT[:, :, tk].rearrange("h p t -> p h t"), reads=[db("hm", 2 * tb2), db("hm", 2 * tb2 + 1)])
            for dc in range(KC):
                a_, m_ = wa[dc % 2], wm[dc % 2]
                LD("pool", a_, a_.t[:], w_up_a[l, :, dc * 128:(dc + 1) * 128].rearrange("(k p) n -> p k n", p=128))
                LD("pool", m_, m_.t[:], w_up_m[l, :, dc * 128:(dc + 1) * 128].rearrange("(k p) n -> p k n", p=128))
                for h in range(2):
                    tb = tb2 * 2 + h
                    ba = (cnt * 2) % 6
                    bm = ba + 1
                    g1, g2, u1, u2 = ga[cnt % 2], gm[cnt % 2], t1[cnt % 2], t2[cnt % 2]
                    cnt += 1
                    LD("sp", g1, g1.t[:], gaT[dc, :, tb * 512:(tb + 1) * 512], reads=[db("ga", tb)])
                    LD("sp", g2, g2.t[:], gmT[dc, :, tb * 512:(tb + 1) * 512], reads=[db("gm", tb)])
                    for k in range(8):
                        MM(PS[:, ba, :], a_.t[:, k, :], at.t[:, k, h * 512:(h + 1) * 512], k == 0, k == 7, [a_.b, at.b], [pb[ba]])
                    for k in range(8):
                        MM(PS[:, bm, :], m_.t[:, k, :], hm_.t[:, k, h * 512:(h + 1) * 512], k == 0, k == 7, [m_.b, hm_.b], [pb[bm]])
                    TT("dve", u1.t[:], PS[:, ba, :], g1.t[:], ALU.mult, [pb[ba], g1.b], [u1.b])
                    TT("dve", u2.t[:], PS[:, bm, :], g2.t[:], ALU.mult, [pb[bm], g2.b], [u2.b])
                    TT("dve", mg.t[:, dc, h * 512:(h + 1) * 512], u1.t[:], u2.t[:], ALU.add, [u1.b, u2.b], [mg.b])
            for d in range(KC):
                w = wo[d % 2]
                LD("pool", w, w.t[:], w_out[l, :, d * 128:(d + 1) * 128].rearrange("(k p) n -> p k n", p=128))
                for h in range(2):
                    bank = 6 + (cnt % 2)
                    for k in range(KC):
                        MM(PS[:, bank, :], w.t[:, k, :], mg.t[:, k, h * 512:(h + 1) * 512], k == 0, k == KC - 1,
                           [w.b, mg.b], [pb[bank]])
                    resid_epilogue(1, d, tb2 * 2 + h, bank, xold[cnt % 2], xnew[cnt % 2])
                    cnt += 1
        allb = [at.b, hm_.b, mg.b] + [t.b for t in wa + wm + wo + ga + gm + t1 + t2 + xold + xnew] + pb
        P.barrier(allb)
        ar.reset(m)

    def stage_out():
        m = ar.mark()
        xi = [ar.alloc([128, KC, 128], F32) for _ in range(2)]
        xo = [ar.alloc([128, D], F32) for _ in range(2)]
        ob = Buf("outbuf")
        for tt in range(T // 128):
            a = xi[tt % 2]
            o = xo[tt % 2]
            LD("sp", a, a.t[:], xT[:, :, tt * 128:(tt + 1) * 128].rearrange("k p t -> p k t"), reads=xbufs(tt // 4))
            for q in range(4):
                bank = (tt * 4 + q) % 8
                for i in range(4):
                    k = q * 4 + i
                    TR(PS[:, bank, i * 128:(i + 1) * 128], a.t[:, k, :], ident.t[:], [a.b, ident.b], [pb[bank]])
                ceng = "act" if q % 2 else "dve"
                cin = PS[:, bank, :]
                cout = o.t[:, q * 512:(q + 1) * 512]
                P.op(ceng, (lambda e, cout=cout, cin=cin: e.copy(out=cout, in_=cin)) if ceng == "act" else
                     (lambda e, cout=cout, cin=cin: e.tensor_copy(out=cout, in_=cin)), [pb[bank]],
                     [o.b] if q == 0 else [], [] if q == 0 else [o.b])
            ST("sp", o, out[tt * 128:(tt + 1) * 128, :], o.t[:], pwrites=[ob])
        P.op("sp", None, reads=[ob])
        ar.reset(m)

    seq = [("in", None)]
    for l in range(depth):
        seq += [("mod", l), ("ffn1", l), ("proj", l), ("attn", l), ("mlstm", l), ("merge", l), ("ffn2", l)]
    for name, l in seq:
        if name == "in":
            stage_in()
        elif name == "mod":
            stage_mod(l)
        elif name == "ffn1":
            stage_ffn(l, 0)
        elif name == "proj":
            stage_proj(l)
        elif name == "attn":
            stage_attn(l)
        elif name == "mlstm":
            stage_mlstm(l)
        elif name == "merge":
            stage_merge(l)
        elif name == "ffn2":
            stage_ffn(l, 2)
        if stop is not None and (name, l if l is not None else 0) == stop:
            break
    stage_out()
    fence_bufs = [b for b in dbufs.values()]
    P.op("sp", None, reads=fence_bufs)
    with es:
        info = P.emit()
    return nc, info, ar.peak


def _pk(v):
    v = np.asarray(v)
    return np.ascontiguousarray(np.swapaxes(v.reshape(v.shape[:-1] + (v.shape[-1] // 128, 128)), -1, -2))


def _rel_index():
    kp = np.arange(128)[:, None, None]
    kt = np.arange(5)[None, :, None]
    q = np.arange(128)[None, None, :]
    rel = 512 + q - kt * 128 - kp
    idx = np.clip(rel, -63, 256) + 63
    qc = q // 64
    kc = (kt * 128 + kp) // 64
    valid = (kc >= qc) & (kc <= qc + 8)
    return idx, valid


def prep_inputs(inputs, b):
    f = lambda a: np.ascontiguousarray(np.asarray(a, dtype=np.float32))
    idx, valid = _rel_index()
    rel = np.asarray(inputs["rel_table"], dtype=np.float32)
    relb = rel[:, :, idx]
    relb = np.ascontiguousarray(np.transpose(relb, (0, 2, 1, 3, 4)))
    amask = np.where(valid, np.float32(0.0), np.float32(-30000.0)).astype(np.float32)
    amask = np.ascontiguousarray(np.broadcast_to(amask, (128, 5, 128)))
    bifv = np.asarray(inputs["b_if"], dtype=np.float32)
    bif = np.ascontiguousarray(np.stack([bifv[:, :4], bifv[:, 4:]], axis=-1))
    convw = np.asarray(inputs["conv_w"], dtype=np.float32)
    convw = np.ascontiguousarray(np.transpose(convw.reshape(2, 4, 16, 128), (0, 3, 2, 1)))
    qkg = np.ascontiguousarray(np.stack([np.asarray(inputs["q_norm_g"]), np.asarray(inputs["k_norm_g"])], axis=-1).astype(np.float32))
    m = {
        "x": f(inputs["x"][b]),
        "cvec": _pk(f(inputs["c"][b])),
        "normg": _pk(f(inputs["norm_g"])),
        "w_ada": f(inputs["w_ada"]),
        "b_ada": np.ascontiguousarray(np.swapaxes(f(inputs["b_ada"]).reshape(2, 144, 128), 1, 2)),
        "ffn1_w1": f(inputs["ffn1_w1"]), "ffn1_w3": f(inputs["ffn1_w3"]), "ffn1_w2": f(inputs["ffn1_w2"]),
        "ffn2_w1": f(inputs["ffn2_w1"]), "ffn2_w3": f(inputs["ffn2_w3"]), "ffn2_w2": f(inputs["ffn2_w2"]),
        "w_in": f(inputs["w_in"]),
        "bif": bif,
        "convw": convw,
        "convb": _pk(f(inputs["conv_b"])),
        "qkg": qkg,
        "relb": relb,
        "amask": amask,
        "mng": _pk(f(inputs["m_norm_g"])),
        "w_up_a": f(inputs["w_up_a"]), "w_up_m": f(inputs["w_up_m"]), "w_out": f(inputs["w_out"]),
    }
    return m


_CACHE = {}
ACTIVE = (0, 2, 4, 6)


def kernel(**inputs):
    if "nc" not in _CACHE:
        _CACHE["nc"] = build()[0]
    nc = _CACHE["nc"]
    shared = prep_inputs(inputs, 0)
    zeros = {k: np.zeros_like(v) for k, v in shared.items()}
    in_maps = []
    for core in range(8):
        if core in ACTIVE:
            b = ACTIVE.index(core)
            m = dict(shared)
            m["x"] = np.ascontiguousarray(np.asarray(inputs["x"][b], dtype=np.float32))
            m["cvec"] = _pk(np.asarray(inputs["c"][b], dtype=np.float32))
        else:
            m = zeros
        in_maps.append(m)
    res = run_bass_kernel_spmd(nc, in_maps, core_ids=list(range(8)))
    outs = [res.results[ACTIVE[b]]["out"] for b in range(4)]
    return np.stack(outs, axis=0).astype(np.float32)
```

```python
import contextlib
import numpy as np
import concourse.bass as bass
import concourse.mybir as mybir
from concourse.bass_utils import run_bass_kernel_spmd

F32 = mybir.dt.float32
BF16 = mybir.dt.bfloat16
AF = mybir.ActivationFunctionType
ALU = mybir.AluOpType

T = 4096
D = 2048
KC = 16
FF = 5504
FC = 43
DIN = 11272
HA = 8
HM = 4
DEPTH = 2
EPS = 1e-6
NB = T // 512
SB_BASE = 16512
SB_END = 229376
ENGS = ("pe", "act", "dve", "pool", "sp")


class Buf:
    __slots__ = ("name", "writers", "readers", "dslot", "epoch")

    def __init__(self, name):
        self.name = name
        self.writers = []
        self.readers = []
        self.dslot = None
        self.epoch = -1


class SemSlot:
    __slots__ = ("sem", "count")

    def __init__(self):
        self.sem = None
        self.count = 0


class Op:
    __slots__ = ("eng", "fn", "deps", "hard", "is_dma", "dbuf", "dslot", "milestone", "dma_wait_vals")

    def __init__(self, eng, fn):
        self.eng = eng
        self.fn = fn
        self.deps = set()
        self.hard = set()
        self.is_dma = False
        self.dbuf = None
        self.dslot = None
        self.milestone = None
        self.dma_wait_vals = {}


class Prog:
    def __init__(self, nc):
        self.nc = nc
        self.ops = []
        self.bar = None
        self.slots = []
        self.slot_i = 0
        self.epoch = 0

    def _add(self, op, reads, writes, pwrites):
        oid = len(self.ops)
        if self.bar is not None:
            op.deps.add(self.bar)
        for b in reads:
            op.deps.update(b.writers)
            op.hard.update(b.writers)
        for b in writes:
            op.deps.update(b.writers)
            op.hard.update(b.writers)
            op.deps.update(b.readers)
        for b in pwrites:
            op.deps.update(b.readers)
        for d in op.deps:
            dop = self.ops[d]
            if dop.is_dma:
                op.dma_wait_vals[d] = dop.dslot.count
        self.ops.append(op)
        for b in reads:
            b.readers.append(oid)
        for b in writes:
            b.writers = [oid]
            b.readers = []
        for b in pwrites:
            b.writers.append(oid)
        return oid

    def op(self, eng, fn, reads=(), writes=(), pwrites=()):
        return self._add(Op(eng, fn), reads, writes, pwrites)

    def dma(self, eng, fn, sbuf, reads=(), writes=(), pwrites=(), n=1):
        o = Op(eng, fn)
        o.is_dma = True
        o.dbuf = sbuf
        if sbuf.dslot is None or sbuf.epoch != self.epoch:
            if self.slot_i >= len(self.slots):
                self.slots.append(SemSlot())
            sbuf.dslot = self.slots[self.slot_i]
            self.slot_i += 1
            sbuf.epoch = self.epoch
        o.dslot = sbuf.dslot
        oid = self._add(o, reads, writes, pwrites)
        o.dslot.count += n
        return oid

    def barrier(self, bufs, newbufs=()):
        oid = self.op("sp", None, writes=list(bufs))
        self.bar = oid
        self.epoch += 1
        self.slot_i = 0
        return oid

    def emit(self):
        nc = self.nc
        ops = self.ops
        need_ms = [False] * len(ops)
        for o in ops:
            for d in o.deps:
                dop = ops[d]
                if (not dop.is_dma) and (dop.eng != o.eng or (o.eng != "pe" and d in o.hard)):
                    need_ms[d] = True
        counters = {e: 0 for e in ENGS}
        for i, o in enumerate(ops):
            if need_ms[i]:
                counters[o.eng] += 1
                o.milestone = counters[o.eng]
        stack = contextlib.ExitStack()
        with stack:
            esem = {e: stack.enter_context(nc.semaphore("ms_" + e)) for e in ENGS}
            nd = len(self.slots)
            for si, sl_ in enumerate(self.slots):
                sl_.sem = stack.enter_context(nc.semaphore("dsem%d" % si))
            per_eng = {e: [] for e in ENGS}
            for i, o in enumerate(ops):
                per_eng[o.eng].append(i)
            block = stack.enter_context(nc.Block())

            def run_engine(ename, eobj):
                seen = {}
                for i in per_eng[ename]:
                    o = ops[i]
                    w = {}
                    for d in o.deps:
                        dop = ops[d]
                        if dop.is_dma:
                            key = ("d", id(dop.dslot))
                            val = 16 * o.dma_wait_vals[d]
                            sem = dop.dslot.sem
                        else:
                            if dop.eng == ename and (ename == "pe" or d not in o.hard):
                                continue
                            key = ("e", dop.eng)
                            val = dop.milestone
                            sem = esem[dop.eng]
                        if seen.get(key, 0) >= val:
                            continue
                        if key not in w or w[key][1] < val:
                            w[key] = (sem, val)
                    for key, (sem, val) in w.items():
                        eobj.wait_ge(sem, val)
                        seen[key] = val
                    if o.fn is None:
                        if o.milestone is not None:
                            eobj.nop().then_inc(esem[ename], 1)
                        continue
                    if o.is_dma:
                        o.fn(eobj, o.dslot.sem)
                    else:
                        ins = o.fn(eobj)
                        if o.milestone is not None:
                            ins.then_inc(esem[ename], 1)

            @block.tensor
            def _(e):
                run_engine("pe", e)

            @block.scalar
            def _(e):
                run_engine("act", e)

            @block.vector
            def _(e):
                run_engine("dve", e)

            @block.gpsimd
            def _(e):
                run_engine("pool", e)

            @block.sync
            def _(e):
                run_engine("sp", e)
        return counters, nd


class Tl:
    _n = 0

    def __init__(self, t, name=None):
        Tl._n += 1
        self.t = t
        self.b = Buf(name or ("t%d" % Tl._n))


class Arena:
    def __init__(self, nc):
        self.nc = nc
        self.off = SB_BASE
        self.n = 0
        self.peak = 0

    def alloc(self, shape, dt, name=None):
        esz = 4 if dt == F32 else 2
        nbytes = int(np.prod(shape[1:])) * esz
        nbytes = (nbytes + 63) // 64 * 64
        assert self.off + nbytes <= SB_END, ("SBUF overflow", self.off, nbytes)
        self.n += 1
        t = self.nc.alloc_sbuf_tensor_at("sb%d" % self.n, list(shape), dt, offset=self.off)
        self.off += nbytes
        self.peak = max(self.peak, self.off)
        return Tl(t, name or ("sb%d" % self.n))

    def mark(self):
        return self.off

    def reset(self, m):
        self.off = m


def build(dbg=(), stop=None, depth=DEPTH):
    nc = bass.Bass("TRN2", target_bir_lowering=False)
    P = Prog(nc)
    ar = Arena(nc)
    es = contextlib.ExitStack()

    def din(name, shape):
        return nc.dram_tensor(name, list(shape), F32, kind="ExternalInput").ap()

    x_in = din("x", [T, D])
    cvec = din("cvec", [128, KC])
    normg = din("normg", [DEPTH, 3, 128, KC])
    w_ada = din("w_ada", [DEPTH, D, 9 * D])
    b_ada = din("b_ada", [DEPTH, 128, 144])
    fw1 = [din("ffn1_w1", [DEPTH, D, FF]), din("ffn2_w1", [DEPTH, D, FF])]
    fw3 = [din("ffn1_w3", [DEPTH, D, FF]), din("ffn2_w3", [DEPTH, D, FF])]
    fw2 = [din("ffn1_w2", [DEPTH, FF, D]), din("ffn2_w2", [DEPTH, FF, D])]
    w_in = din("w_in", [DEPTH, D, DIN])
    bif = din("bif", [DEPTH, 4, 2])
    convw = din("convw", [DEPTH, 128, 16, 4])
    convb = din("convb", [DEPTH, 128, 16])
    qkg = din("qkg", [DEPTH, 128, 2])
    relb = din("relb", [DEPTH, 128, HA, 5, 128])
    amask = din("amask", [128, 5, 128])
    mng = din("mng", [DEPTH, 128, 8])
    w_up_a = din("w_up_a", [DEPTH, 1024, D])
    w_up_m = din("w_up_m", [DEPTH, 1024, D])
    w_out = din("w_out", [DEPTH, D, D])
    out = nc.dram_tensor("out", [T, D], F32, kind="ExternalOutput").ap()

    def scratch(name, shape, dt):
        kind = "ExternalOutput" if name in dbg else "Internal"
        return nc.dram_tensor(name, list(shape), dt, kind=kind).ap()

    xT = scratch("xT", [KC, 128, T], F32)
    qaT = scratch("qaT", [HA, 128, T], BF16)
    kaT = scratch("kaT", [HA, 128, T], BF16)
    va = scratch("va", [T, 1024], BF16)
    qkmT = scratch("qkmT", [16, 128, T], BF16)
    vm = scratch("vm", [T, 1024], BF16)
    omT = scratch("omT", [8, 128, T], BF16)
    igT = scratch("igT", [4, T], F32)
    lfT = scratch("lfT", [4, T], F32)
    gaT = scratch("gaT", [KC, 128, T], BF16)
    gmT = scratch("gmT", [KC, 128, T], BF16)
    attnT = scratch("attnT", [HA, 128, T], BF16)
    hmT = scratch("hmT", [8, 128, T], BF16)
    rowsD = scratch("rowsD", [3, 4, T], F32)

    dbufs = {}

    def db(name, *idx):
        k = (name,) + idx
        if k not in dbufs:
            dbufs[k] = Buf("dr_" + "_".join(str(i) for i in k))
        return dbufs[k]

    PS = es.enter_context(nc.psum_tensor("PS", [128, 8, 512], F32))
    PSB = PS.bitcast(BF16) if hasattr(PS, "bitcast") else None
    pb = [Buf("psb%d" % i) for i in range(8)]

    def MM(o, lhsT, rhs, start, stop, reads, writes):
        P.op("pe", lambda e: e.matmul(o, lhsT=lhsT, rhs=rhs, start=start, stop=stop), reads, writes)

    def TR(o, in_, ident_ap, reads, writes):
        P.op("pe", lambda e: e.transpose(out=o, in_=in_, identity=ident_ap), reads, writes)

    def ACTF(o, in_, func, reads, writes, bias=None, scale=None):
        kw = {}
        if bias is not None:
            kw["bias"] = bias
        if scale is not None:
            kw["scale"] = scale
        P.op("act", lambda e: e.activation(out=o, in_=in_, func=func, **kw), reads, writes)

    def TT(eng, o, in0, in1, op, reads, writes):
        P.op(eng, lambda e: e.tensor_tensor(out=o, in0=in0, in1=in1, op=op), reads, writes)

    def TS(eng, o, in0, s1, s2, op0, op1, reads, writes):
        if s2 is None:
            P.op(eng, lambda e: e.tensor_scalar(out=o, in0=in0, scalar1=s1, scalar2=None, op0=op0), reads, writes)
        else:
            P.op(eng, lambda e: e.tensor_scalar(out=o, in0=in0, scalar1=s1, scalar2=s2, op0=op0, op1=op1), reads, writes)

    def STT(eng, o, in0, scalar, in1, op0, op1, reads, writes):
        P.op(eng, lambda e: e.scalar_tensor_tensor(out=o, in0=in0, scalar=scalar, in1=in1, op0=op0, op1=op1), reads, writes)

    def CP(eng, o, in_, reads, writes):
        if eng == "act":
            P.op("act", lambda e: e.copy(out=o, in_=in_), reads, writes)
        else:
            P.op(eng, lambda e: e.tensor_copy(out=o, in_=in_), reads, writes)

    def RECIP(o, in_, reads, writes):
        P.op("dve", lambda e: e.reciprocal(out=o, in_=in_), reads, writes)

    def MEMSET(eng, o, val, writes):
        P.op(eng, lambda e: e.memset(o, val), (), writes)

    def LD(eng, tl, o, in_, reads=(), nc_ok=False):
        if nc_ok:
            P.dma(eng, lambda e, s: e.dma_start(out=o, in_=in_, allow_slow_non_contiguous=True).then_inc(s, 16), tl.b,
                  reads=list(reads), writes=[tl.b])
        else:
            P.dma(eng, lambda e, s: e.dma_start(out=o, in_=in_).then_inc(s, 16), tl.b, reads=list(reads), writes=[tl.b])

    def ST(eng, tl, o, in_, writes=(), pwrites=()):
        P.dma(eng, lambda e, s: e.dma_start(out=o, in_=in_).then_inc(s, 16), tl.b, reads=[tl.b],
              writes=list(writes), pwrites=list(pwrites))

    ident = ar.alloc([128, 128], F32, "ident")
    identb = ar.alloc([128, 128], BF16, "identb")
    onesb = ar.alloc([128, 128], BF16, "onesb")
    tri = ar.alloc([128, 128], F32, "tri")
    sel = ar.alloc([4, 4, 128], F32, "sel")
    epsT = ar.alloc([128, 1], F32, "eps")
    cact = ar.alloc([128, KC], F32, "cact")
    cactb = ar.alloc([128, KC], BF16, "cactb")
    modv = ar.alloc([128, 144], F32, "modv")
    Amod = ar.alloc([128, 3, KC], F32, "Amod")
    gateh = ar.alloc([128, 3, KC], F32, "gateh")
    ngt = ar.alloc([128, 3, KC], F32, "ngt")
    bad = ar.alloc([128, 144], F32, "bad")
    cwT = ar.alloc([128, 16, 4], F32, "cw")
    cbT = ar.alloc([128, 16], F32, "cb")
    qkgT = ar.alloc([128, 2], F32, "qkg")
    mngT = ar.alloc([128, 8], F32, "mng")
    bifT = ar.alloc([4, 2], F32, "bif")
    nbfT = ar.alloc([4, 1], F32, "nbf")
    halo = ar.alloc([128, 16, 3], F32, "halo")

    MEMSET("pool", ident.t[:], 1.0, [ident.b])
    P.op("pool", lambda e: e.affine_select(out=ident.t[:], in_=ident.t[:], pattern=[[-1, 128]], compare_op=ALU.is_equal,
                                           fill=0.0, base=0, channel_multiplier=1), [ident.b], [ident.b])
    CP("pool", identb.t[:], ident.t[:], [ident.b], [identb.b])
    MEMSET("pool", onesb.t[:], 1.0, [onesb.b])
    MEMSET("pool", tri.t[:], 1.0, [tri.b])
    P.op("pool", lambda e: e.affine_select(out=tri.t[:], in_=tri.t[:], pattern=[[1, 128]], compare_op=ALU.is_ge,
                                           fill=0.0, base=0, channel_multiplier=-1), [tri.b], [tri.b])
    MEMSET("pool", sel.t[:], 1.0, [sel.b])
    P.op("pool", lambda e: e.affine_select(out=sel.t[:], in_=sel.t[:], pattern=[[1, 4], [0, 128]], compare_op=ALU.is_equal,
                                           fill=0.0, base=0, channel_multiplier=-1), [sel.b], [sel.b])
    MEMSET("pool", epsT.t[:], EPS, [epsT.b])
    LD("sp", cact, cact.t[:], cvec)
    ACTF(cact.t[:], cact.t[:], AF.Silu, [cact.b], [cact.b])
    CP("dve", cactb.t[:], cact.t[:], [cact.b], [cactb.b])
    persist_mark = ar.mark()

    def xbufs(tb, ks=range(KC)):
        return [db("xT", k, tb) for k in ks]

    def stage_in():
        m = ar.mark()
        xin = [ar.alloc([128, D], F32) for _ in range(2)]
        xo = [ar.alloc([128, KC, 128], F32) for _ in range(2)]
        for tt in range(T // 128):
            a = xin[tt % 2]
            o = xo[tt % 2]
            LD("sp", a, a.t[:], x_in[tt * 128:(tt + 1) * 128, :])
            for q in range(4):
                bank = (tt * 4 + q) % 8
                for i in range(4):
                    k = q * 4 + i
                    TR(PS[:, bank, i * 128:(i + 1) * 128], a.t[:, k * 128:(k + 1) * 128], ident.t[:],
                       [a.b, ident.b], [pb[bank]])
                ceng = "act" if q % 2 else "dve"
                cin = PS[:, bank, :].rearrange("p (i t) -> p i t", i=4)
                cout = o.t[:, q * 4:(q + 1) * 4, :]
                P.op(ceng, (lambda e, cout=cout, cin=cin: e.copy(out=cout, in_=cin)) if ceng == "act" else
                     (lambda e, cout=cout, cin=cin: e.tensor_copy(out=cout, in_=cin)), [pb[bank]],
                     [o.b] if q == 0 else [], [] if q == 0 else [o.b])
            ST("sp", o, xT[:, :, tt * 128:(tt + 1) * 128].rearrange("k p t -> p k t"), o.t[:],
               pwrites=xbufs(tt // 4))
        P.barrier([t.b for t in xin + xo] + pb)
        ar.reset(m)

    def stage_mod(l):
        m = ar.mark()
        wt = [ar.alloc([128, KC, 512], BF16) for _ in range(2)]
        LD("sp", bad, bad.t[:], b_ada[l])
        LD("sp", ngt, ngt.t[:], normg[l].rearrange("i p k -> p i k"))
        LD("sp", cwT, cwT.t[:], convw[l])
        LD("sp", cbT, cbT.t[:], convb[l])
        LD("sp", qkgT, qkgT.t[:], qkg[l])
        LD("sp", mngT, mngT.t[:], mng[l])
        LD("sp", bifT, bifT.t[:], bif[l])
        TS("dve", qkgT.t[:, 0:1], qkgT.t[:, 0:1], float(128 ** -0.5), None, ALU.mult, None, [qkgT.b], [qkgT.b])
        TS("dve", nbfT.t[:], bifT.t[:, 1:2], -1.0, None, ALU.mult, None, [bifT.b], [nbfT.b])
        for blk in range(36):
            w = wt[blk % 2]
            LD("pool", w, w.t[:], w_ada[l, :, blk * 512:(blk + 1) * 512].rearrange("(k p) n -> p k n", p=128))
            for c4 in range(4):
                j = blk * 4 + c4
                for k in range(KC):
                    MM(PS[:, 0, j:j + 1], w.t[:, k, c4 * 128:(c4 + 1) * 128], cactb.t[:, k:k + 1], k == 0, k == KC - 1,
                       [w.b, cactb.b], [pb[0]])
        TT("dve", modv.t[:], PS[:, 0, 0:144], bad.t[:], ALU.add, [pb[0], bad.b], [modv.b])
        mv = modv.t[:].rearrange("p (j k) -> p j k", j=9)
        for i in range(3):
            STT("dve", Amod.t[:, i, :], mv[:, 3 * i + 1, :], 1.0, ngt.t[:, i, :], ALU.add, ALU.mult,
                [modv.b, ngt.b], [Amod.b])
            TS("dve", gateh.t[:, i, :], mv[:, 3 * i + 2, :], 0.5 if i != 1 else 1.0, None, ALU.mult, None,
               [modv.b], [gateh.b])
        P.barrier([t.b for t in wt] + [pb[0]])
        ar.reset(m)

    def shift_ap(i, k):
        return modv.t[:, (3 * i) * KC + k:(3 * i) * KC + k + 1]

    def norm_block(i, tb, xm, col0, xin, sq, rs, tmp, bank):
        LD("sp", xin, xin.t[:], xT[:, :, tb * 512:(tb + 1) * 512].rearrange("k p t -> p k t"), reads=xbufs(tb))
        for g4 in range(4):
            s = sq[g4 % 2]
            ACTF(s.t[:], xin.t[:, g4 * 4:(g4 + 1) * 4, :], AF.Square, [xin.b], [s.b])
            for kk in range(4):
                k = g4 * 4 + kk
                MM(PS[:, bank, :], onesb.t[:], s.t[:, kk, :], k == 0, k == KC - 1, [onesb.b, s.b], [pb[bank]])
        ACTF(rs.t[:], PS[:, bank, :], AF.Sqrt, [pb[bank], epsT.b], [rs.b], bias=epsT.t[:, 0:1], scale=1.0 / D)
        RECIP(rs.t[:], rs.t[:], [rs.b], [rs.b])
        for k in range(KC):
            t_ = tmp[k % 2]
            STT("dve", t_.t[:], xin.t[:, k, :], Amod.t[:, i, k:k + 1], rs.t[:], ALU.mult, ALU.mult,
                [xin.b, Amod.b, rs.b], [t_.b])
            ACTF(xm.t[:, k, col0:col0 + 512], t_.t[:], AF.Identity, [t_.b, modv.b], [xm.b], bias=shift_ap(i, k))

    def resid_epilogue(i, d, tb, bank, xold, xnew):
        LD("sp", xold, xold.t[:], xT[d, :, tb * 512:(tb + 1) * 512], reads=[db("xT", d, tb)])
        STT("dve", xnew.t[:], PS[:, bank, :], gateh.t[:, i, d:d + 1], xold.t[:], ALU.mult, ALU.add,
            [pb[bank], gateh.b, xold.b], [xnew.b])
        ST("sp", xnew, xT[d, :, tb * 512:(tb + 1) * 512], xnew.t[:], writes=[db("xT", d, tb)])

    def stage_ffn(l, i):
        fi = 0 if i == 0 else 1
        w1d, w3d, w2d = fw1[fi], fw3[fi], fw2[fi]
        m = ar.mark()
        xm = ar.alloc([128, KC, 1024], BF16, "xm")
        g = ar.alloc([128, FC, 1024], BF16, "g")
        xp = [ar.alloc([128, KC, 128], F32) for _ in range(2)]
        sqp = [ar.alloc([128, KC, 128], BF16) for _ in range(2)]
        rp = [ar.alloc([128, 128], F32) for _ in range(2)]
        w13 = [[ar.alloc([128, KC, 128], BF16) for _ in range(3)] for _ in range(2)]
        w2 = [ar.alloc([128, FC, 128], BF16) for _ in range(2)]
        sil = [ar.alloc([128, 512], F32) for _ in range(2)]
        xold = [ar.alloc([128, 512], F32) for _ in range(2)]
        xnew = [ar.alloc([128, 512], F32) for _ in range(2)]

        def piece_front(q):
            tb2, pc = divmod(q, 8)
            t0 = tb2 * 1024 + pc * 128
            x_, s_ = xp[q % 2], sqp[q % 2]
            LD("sp", x_, x_.t[:], xT[:, :, t0:t0 + 128].rearrange("k p t -> p k t"), reads=xbufs(t0 // 512))
            ACTF(s_.t[:], x_.t[:], AF.Square, [x_.b], [s_.b])

        def piece_back(q):
            tb2, pc = divmod(q, 8)
            x_, s_, r_ = xp[q % 2], sqp[q % 2], rp[q % 2]
            c0 = (q % 4) * 128
            for k in range(KC):
                MM(PS[:, 0, c0:c0 + 128], onesb.t[:], s_.t[:, k, :], k == 0, k == KC - 1, [onesb.b, s_.b], [pb[0]])
            ACTF(r_.t[:], PS[:, 0, c0:c0 + 128], AF.Sqrt, [pb[0], epsT.b], [r_.b], bias=epsT.t[:, 0:1], scale=1.0 / D)
            RECIP(r_.t[:], r_.t[:], [r_.b], [r_.b])
            TT("dve", x_.t[:], x_.t[:], r_.t[:, None, :].to_broadcast([128, KC, 128]), ALU.mult, [x_.b, r_.b], [x_.b])
            for k in range(KC):
                ACTF(xm.t[:, k, pc * 128:(pc + 1) * 128], x_.t[:, k, :], AF.Identity, [x_.b, Amod.b, modv.b], [xm.b],
                     bias=shift_ap(i, k), scale=Amod.t[:, i, k:k + 1])

        cnt = 0
        nblk = T // 1024
        piece_front(0)
        for pc in range(8):
            if pc + 1 < 8:
                piece_front(pc + 1)
            piece_back(pc)
        for tb2 in range(nblk):
            for f in range(FC):
                wa = w13[0][f % 3]
                wb = w13[1][f % 3]
                LD("pool", wa, wa.t[:], w1d[l, :, f * 128:(f + 1) * 128].rearrange("(k p) n -> p k n", p=128))
                LD("pool", wb, wb.t[:], w3d[l, :, f * 128:(f + 1) * 128].rearrange("(k p) n -> p k n", p=128))
                for h in range(2):
                    b1 = (cnt * 2) % 6
                    b3 = b1 + 1
                    for k in range(KC):
                        MM(PS[:, b1, :], wa.t[:, k, :], xm.t[:, k, h * 512:(h + 1) * 512], k == 0, k == KC - 1,
                           [wa.b, xm.b], [pb[b1]])
                    for k in range(KC):
                        MM(PS[:, b3, :], wb.t[:, k, :], xm.t[:, k, h * 512:(h + 1) * 512], k == 0, k == KC - 1,
                           [wb.b, xm.b], [pb[b3]])
                    s_ = sil[cnt % 2]
                    ACTF(s_.t[:], PS[:, b1, :], AF.Silu, [pb[b1]], [s_.b])
                    TT("dve", g.t[:, f, h * 512:(h + 1) * 512], s_.t[:], PS[:, b3, :], ALU.mult,
                       [s_.b, pb[b3]], [g.b])
                    cnt += 1
            nxt = tb2 + 1 < nblk
            for d in range(KC):
                if nxt and d % 2 == 0:
                    piece_front((tb2 + 1) * 8 + d // 2)
                w = w2[d % 2]
                LD("pool", w, w.t[:], w2d[l, :, d * 128:(d + 1) * 128].rearrange("(f p) n -> p f n", p=128))
                for h in range(2):
                    bank = 6 + (cnt % 2)
                    for f in range(FC):
                        MM(PS[:, bank, :], w.t[:, f, :], g.t[:, f, h * 512:(h + 1) * 512], f == 0, f == FC - 1,
                           [w.b, g.b], [pb[bank]])
                    resid_epilogue(i, d, tb2 * 2 + h, bank, xold[cnt % 2], xnew[cnt % 2])
                    cnt += 1
                if nxt and d % 2 == 1:
                    piece_back((tb2 + 1) * 8 + d // 2)
        allb = [xm.b, g.b] + [t.b for t in xp + sqp + rp + w13[0] + w13[1] + w2 + sil + xold + xnew] + pb
        P.barrier(allb)
        ar.reset(m)

    def stage_proj(l):
        m = ar.mark()
        xm = ar.alloc([128, KC, 1024], BF16, "xm")
        xin = ar.alloc([128, KC, 512], F32)
        sq = [ar.alloc([128, 4, 512], BF16) for _ in range(2)]
        rs = ar.alloc([128, 512], F32)
        tmp = [ar.alloc([128, 512], F32) for _ in range(2)]
        wt = [ar.alloc([128, KC, 128], BF16) for _ in range(3)]
        wv = [ar.alloc([128, KC, 512], BF16) for _ in range(2)]
        wif = ar.alloc([128, KC, 8], BF16)
        osb = [ar.alloc([128, 512], BF16) for _ in range(4)]
        sqb = [ar.alloc([128, 512], BF16) for _ in range(2)]
        rs2 = [ar.alloc([128, 512], F32) for _ in range(2)]
        uext = [ar.alloc([128, 515], F32) for _ in range(2)]
        acc = [ar.alloc([128, 512], F32) for _ in range(2)]
        rows = [ar.alloc([4, 512], F32) for _ in range(4)]
        MEMSET("dve", halo.t[:], 0.0, [halo.b])
        LD("pool", wif, wif.t[:], w_in[l, :, 7168:7176].rearrange("(k p) n -> p k n", p=128), nc_ok=True)
        cnt = [0]
        pend = []

        def flush():
            while pend:
                pend.pop(0)()

        def fm_group(c0, tb2, epi):
            w = wt[cnt[0] % 3]
            LD("pool", w, w.t[:], w_in[l, :, c0:c0 + 128].rearrange("(k p) n -> p k n", p=128))
            for h in range(2):
                bank = cnt[0] % 4
                cnt[0] += 1
                for k in range(KC):
                    MM(PS[:, bank, :], w.t[:, k, :], xm.t[:, k, h * 512:(h + 1) * 512], k == 0, k == KC - 1,
                       [w.b, xm.b], [pb[bank]])
                flush()
                epi(bank, tb2 * 2 + h, h)

        def epi_qk(which, head):
            dst = qaT if which == 0 else kaT

            def epi(bank, tb, h):
                n = cnt[0]
                s = sqb[n % 2]
                r = rs2[n % 2]
                o = osb[n % 4]
                b2 = 4 + n % 2
                ACTF(s.t[:], PS[:, bank, :], AF.Square, [pb[bank]], [s.b])

                def later():
                    MM(PS[:, b2, :], onesb.t[:], s.t[:], True, True, [onesb.b, s.b], [pb[b2]])
                    ACTF(r.t[:], PS[:, b2, :], AF.Sqrt, [pb[b2], epsT.b], [r.b], bias=epsT.t[:, 0:1], scale=1.0 / 128)
                    RECIP(r.t[:], r.t[:], [r.b], [r.b])
                    STT("dve", o.t[:], PS[:, bank, :], qkgT.t[:, which:which + 1], r.t[:], ALU.mult, ALU.mult,
                        [pb[bank], qkgT.b, r.b], [o.b])
                    ST("sp", o, dst[head, :, tb * 512:(tb + 1) * 512], o.t[:], pwrites=[db("qk", tb)])
                pend.append(later)
            return epi

        def epi_sig(dst, ch, name):
            def epi(bank, tb, h):
                o = osb[cnt[0] % 4]
                ACTF(o.t[:], PS[:, bank, :], AF.Sigmoid, [pb[bank]], [o.b])
                ST("sp", o, dst[ch, :, tb * 512:(tb + 1) * 512], o.t[:], pwrites=[db(name, tb)])
            return epi

        def epi_conv(ch):
            def epi(bank, tb, h):
                n = cnt[0]
                u = uext[n % 2]
                a = acc[n % 2]
                o = osb[n % 4]
                CP("act", u.t[:, 3:515], PS[:, bank, :], [pb[bank]], [u.b])
                CP("act", u.t[:, 0:3], halo.t[:, ch, :], [halo.b], [u.b])
                CP("act", halo.t[:, ch, :], u.t[:, 512:515], [u.b], [halo.b])
                TS("dve", a.t[:], u.t[:, 0:512], cwT.t[:, ch, 0:1], None, ALU.mult, None, [u.b, cwT.b], [a.b])
                for j in range(1, 4):
                    STT("dve", a.t[:], u.t[:, j:j + 512], cwT.t[:, ch, j:j + 1], a.t[:], ALU.mult, ALU.add,
                        [u.b, cwT.b, a.b], [a.b])
                ACTF(o.t[:], a.t[:], AF.Silu, [a.b, cbT.b], [o.b], bias=cbT.t[:, ch:ch + 1])
                ST("sp", o, qkmT[ch, :, tb * 512:(tb + 1) * 512], o.t[:], pwrites=[db("qkm", tb)])
            return epi

        def tm_block(c0, dst, tb2, name):
            w = wv[cnt[0] % 2]
            LD("pool", w, w.t[:], w_in[l, :, c0:c0 + 512].rearrange("(k p) n -> p k n", p=128))
            cc = (c0 % 1024)
            for tt in range(8):
                bank = cnt[0] % 4
                o = osb[cnt[0] % 4]
                cnt[0] += 1
                for k in range(KC):
                    MM(PS[:, bank, :], xm.t[:, k, tt * 128:(tt + 1) * 128], w.t[:, k, :], k == 0, k == KC - 1,
                       [w.b, xm.b], [pb[bank]])
                flush()
                CP("act" if tt % 2 else "dve", o.t[:], PS[:, bank, :], [pb[bank]], [o.b])
                t0 = tb2 * 1024 + tt * 128
                ST("sp", o, dst[t0:t0 + 128, cc:cc + 512], o.t[:], pwrites=[db(name, t0 // 512)])

        for tb2 in range(T // 1024):
            for h in range(2):
                norm_block(1, tb2 * 2 + h, xm, h * 512, xin, sq, rs, tmp, 7)
            for hd in range(HA):
                fm_group(hd * 128, tb2, epi_qk(0, hd))
            for hd in range(HA):
                fm_group(1024 + hd * 128, tb2, epi_qk(1, hd))
            for cb in range(2):
                tm_block(2048 + cb * 512, va, tb2, "va")
            for ch in range(16):
                fm_group(3072 + ch * 128, tb2, epi_conv(ch))
            for cb in range(2):
                tm_block(5120 + cb * 512, vm, tb2, "vm")
            for ch in range(8):
                fm_group(6144 + ch * 128, tb2, epi_sig(omT, ch, "om"))
            for h in range(2):
                tb = tb2 * 2 + h
                for gi in range(2):
                    bank = cnt[0] % 4
                    cnt[0] += 1
                    r = rows[(h * 2 + gi) % 4]
                    for k in range(KC):
                        MM(PS[0:4, bank, :], wif.t[:, k, gi * 4:(gi + 1) * 4], xm.t[:, k, h * 512:(h + 1) * 512],
                           k == 0, k == KC - 1, [wif.b, xm.b], [pb[bank]])
                    flush()
                    if gi == 0:
                        ACTF(r.t[:], PS[0:4, bank, :], AF.Identity, [pb[bank], bifT.b], [r.b], bias=bifT.t[:, 0:1])
                        ST("sp", r, igT[:, tb * 512:(tb + 1) * 512], r.t[:], pwrites=[db("gates", tb)])
                    else:
                        ACTF(r.t[:], PS[0:4, bank, :], AF.Exp, [pb[bank], nbfT.b], [r.b], bias=nbfT.t[:, 0:1], scale=-1.0)
                        ACTF(r.t[:], r.t[:], AF.Ln, [r.b], [r.b], bias=1.0)
                        TS("dve", r.t[:], r.t[:], -1.0, None, ALU.mult, None, [r.b], [r.b])
                        ST("sp", r, lfT[:, tb * 512:(tb + 1) * 512], r.t[:], pwrites=[db("gates", tb)])
            for ch in range(KC):
                fm_group(7176 + ch * 128, tb2, epi_sig(gaT, ch, "ga"))
            for ch in range(KC):
                fm_group(9224 + ch * 128, tb2, epi_sig(gmT, ch, "gm"))
            flush()
        allb = [xm.b, xin.b, rs.b, wif.b, halo.b] + [t.b for t in sq + tmp + wt + wv + osb + sqb + rs2 + uext + acc + rows] + pb
        P.barrier(allb)
        ar.reset(m)

    def stage_attn(l):
        m = ar.mark()
        bias = ar.alloc([128, HA, 5, 128], F32, "bias")
        msk = ar.alloc([128, 5, 128], F32)
        kwin = [ar.alloc([128, HA, 1024], BF16) for _ in range(2)]
        vwin = [ar.alloc([128, 8, 1024], BF16) for _ in range(2)]
        qblk = [ar.alloc([128, HA, 512], BF16) for _ in range(2)]
        oblk = [ar.alloc([128, HA, 512], BF16) for _ in range(2)]
        sb_ = [ar.alloc([128, 5, 128], F32) for _ in range(2)]
        pT = [ar.alloc([128, 5, 128], BF16) for _ in range(2)]
        rden = [ar.alloc([128, 128], F32) for _ in range(2)]
        LD("sp", bias, bias.t[:], relb[l])
        LD("sp", msk, msk.t[:], amask)
        for hd in range(HA):
            TT("dve", bias.t[:, hd], bias.t[:, hd], msk.t[:], ALU.add, [bias.b, msk.b], [bias.b])
        PS2 = PS[:].rearrange("p (a b) n -> p a (b n)", b=2)
        n = 0
        prev = None
        def load_block(tb):
            kw, vw, qb = kwin[tb % 2], vwin[tb % 2], qblk[tb % 2]
            lo = 0 if tb > 0 else 512
            t_lo = tb * 512 - 512 + lo
            rd = [db("qk", tb), db("va", tb)] + ([db("qk", tb - 1), db("va", tb - 1)] if tb > 0 else [])
            LD("sp", kw, kw.t[:, :, lo:1024], kaT[:, :, t_lo:(tb + 1) * 512].rearrange("h p t -> p h t"), reads=rd)
            LD("sp", vw, vw.t[:, lo // 128:8, :], va[t_lo:(tb + 1) * 512, :].rearrange("(tt p) d -> p tt d", p=128), reads=rd)
            LD("sp", qb, qb.t[:], qaT[:, :, tb * 512:(tb + 1) * 512].rearrange("h p t -> p h t"), reads=rd)

        load_block(0)
        for tb in range(NB):
            kw, vw, qb, ob = kwin[tb % 2], vwin[tb % 2], qblk[tb % 2], oblk[tb % 2]
            if tb + 1 < NB:
                load_block(tb + 1)
            for jj in range(4):
                j = tb * 4 + jj
                kt0 = max(0, 4 - j)
                for hd in range(HA):
                    sreg = n % 2
                    sbufs = [pb[2 * sreg], pb[2 * sreg + 1]]
                    for kt in range(kt0, 5):
                        MM(PS2[:, sreg, kt * 128:(kt + 1) * 128], kw.t[:, hd, (jj + kt) * 128:(jj + kt + 1) * 128],
                           qb.t[:, hd, jj * 128:(jj + 1) * 128], True, True, [kw.b, qb.b], sbufs)
                    s_ = sb_[n % 2]
                    p_ = pT[n % 2]
                    TT("dve", s_.t[:, kt0:5, :], PS2[:, sreg, kt0 * 128:640].rearrange("p (a b) -> p a b", b=128),
                       bias.t[:, hd, kt0:5, :], ALU.add, sbufs + [bias.b], [s_.b])
                    ACTF(p_.t[:, kt0:5, :], s_.t[:, kt0:5, :], AF.Exp, [s_.b], [p_.b])
                    if prev is not None:
                        prev()

                    def pv(p_=p_, vw=vw, ob=ob, hd=hd, jj=jj, kt0=kt0, n=n):
                        obank = 4 + (n % 4)
                        r_ = rden[n % 2]
                        for kt in range(kt0, 5):
                            MM(PS[:, obank, 0:128], vw.t[:, jj + kt, hd * 128:(hd + 1) * 128], p_.t[:, kt, :],
                               kt == kt0, kt == 4, [vw.b, p_.b], [pb[obank]])
                        for kt in range(kt0, 5):
                            MM(PS[:, obank, 128:256], onesb.t[:], p_.t[:, kt, :], kt == kt0, kt == 4,
                               [onesb.b, p_.b], [pb[obank]])
                        RECIP(r_.t[:], PS[:, obank, 128:256], [pb[obank]], [r_.b])
                        TT("dve", ob.t[:, hd, jj * 128:(jj + 1) * 128], PS[:, obank, 0:128], r_.t[:], ALU.mult,
                           [pb[obank], r_.b], [ob.b])
                    prev = pv
                    n += 1
            prev()
            prev = None
            ST("sp", ob, attnT[:, :, tb * 512:(tb + 1) * 512].rearrange("h p t -> p h t"), ob.t[:], writes=[db("attn", tb)])
        allb = [bias.b, msk.b] + [t.b for t in kwin + vwin + qblk + oblk + sb_ + pT + rden] + pb
        P.barrier(allb)
        ar.reset(m)

    LM = 128
    LN16 = float(np.log(1.0 / 16.0))

    def stage_mlstm(l):
        m = ar.mark()
        Cst = ar.alloc([128, HM, 2, 384], F32, "Cst")
        Cbf = ar.alloc([128, HM, 2, 384], BF16, "Cbf")
        qk = [[ar.alloc([128, 4, 512], BF16) for _ in range(2)] for _ in range(HM)]
        vaug = [[ar.alloc([128, 4, 384], BF16) for _ in range(2)] for _ in range(HM)]
        qh = [[ar.alloc([128, 2, 512], BF16) for _ in range(2)] for _ in range(HM)]
        kh = [[ar.alloc([128, 2, 512], BF16) for _ in range(2)] for _ in range(HM)]
        kw_ = [ar.alloc([128, 2, 512], BF16) for _ in range(2)]
        ktok = [[ar.alloc([128, 4, 256], BF16) for _ in range(2)] for _ in range(HM)]
        ebl = [[ar.alloc([128, 4], F32) for _ in range(2)] for _ in range(HM)]
        wT = [ar.alloc([128, 128], BF16) for _ in range(4)]
        hT = [[ar.alloc([128, 2, 512], F32) for _ in range(2)] for _ in range(HM)]
        aden = [ar.alloc([128, 128], F32) for _ in range(2)]
        omb = [ar.alloc([128, 2, 512], BF16) for _ in range(2)]
        sqh = [ar.alloc([128, 2, 512], BF16) for _ in range(2)]
        rsh = [ar.alloc([128, 512], F32) for _ in range(2)]
        hn = [ar.alloc([128, 2, 512], F32) for _ in range(2)]
        hob = [ar.alloc([128, 2, 512], BF16) for _ in range(2)]
        gd = [db("gates", tb) for tb in range(NB)]
        glf = ar.alloc([128, LM], F32, "glf")
        glf2 = ar.alloc([128, LM], F32, "glf2")
        gig = ar.alloc([128, LM], F32, "gig")
        geb = ar.alloc([128, LM], F32, "geb")
        geu = ar.alloc([128, LM], F32, "geu")
        gew = ar.alloc([128, LM], F32, "gew")
        rowt = [ar.alloc([4, 3, 512], F32) for _ in range(2)]
        LD("sp", glf, glf.t[:], lfT.rearrange("h (c t) -> (h c) t", t=LM), reads=gd)
        LD("sp", gig, gig.t[:], igT.rearrange("h (c t) -> (h c) t", t=LM), reads=gd)
        src, dst = glf, glf2
        s = 1
        while s < LM:
            TT("dve", dst.t[:, s:LM], src.t[:, s:LM], src.t[:, 0:LM - s], ALU.add, [src.b], [dst.b])
            CP("dve", dst.t[:, 0:s], src.t[:, 0:s], [src.b], [dst.b])
            src, dst = dst, src
            s *= 2
        bcs, oth = src, dst
        ACTF(geb.t[:], bcs.t[:], AF.Exp, [bcs.b], [geb.b])
        STT("dve", oth.t[:], gig.t[:], LN16, bcs.t[:], ALU.add, ALU.subtract, [gig.b, bcs.b], [oth.b])
        ACTF(geu.t[:], oth.t[:], AF.Exp, [oth.b], [geu.b])
        TS("dve", oth.t[:], oth.t[:], bcs.t[:, LM - 1:LM], None, ALU.add, None, [oth.b, bcs.b], [oth.b])
        ACTF(gew.t[:], oth.t[:], AF.Exp, [oth.b], [gew.b])
        for ri, gt in enumerate((geb, geu, gew)):
            ST("sp", gt, rowsD[ri].rearrange("h (c t) -> (h c) t", t=LM), gt.t[:], writes=[db("rows", ri)])
        MEMSET("dve", Cst.t[:], 0.0, [Cst.b])
        MEMSET("dve", Cbf.t[:], 0.0, [Cbf.b])
        for hd in range(HM):
            for s2 in range(2):
                MEMSET("pool", vaug[hd][s2].t[:, :, 256:384], 1.0, [vaug[hd][s2].b])
        n = [0]
        for tb in range(NB):
            sl = tb % 2
            tok = slice(tb * 512, (tb + 1) * 512)
            rw = rowt[sl]
            LD("sp", rw, rw.t[:], rowsD[:, :, tok].rearrange("r h t -> h r t"), reads=[db("rows", 0), db("rows", 1), db("rows", 2)])
            for hd in range(HM):
                q_ = qk[hd][sl]
                v_ = vaug[hd][sl]
                LD("sp", q_, q_.t[:, 0:2, :], qkmT[2 * hd:2 * hd + 2, :, tok].rearrange("c p t -> p c t"), reads=[db("qkm", tb)])
                LD("sp", q_, q_.t[:, 2:4, :], qkmT[8 + 2 * hd:8 + 2 * hd + 2, :, tok].rearrange("c p t -> p c t"), reads=[db("qkm", tb)])
                LD("sp", v_, v_.t[:, :, 0:256], vm[tok, hd * 256:(hd + 1) * 256].rearrange("(tt p) d -> p tt d", p=128),
                   reads=[db("vm", tb)])
                qh_, kh_, kw2, kt_, eb_ = qh[hd][sl], kh[hd][sl], kw_[hd % 2], ktok[hd][sl], ebl[hd][sl]
                MM(PS[:, 0, :], sel.t[:, hd, :], rw.t[:, 0, :], True, True, [sel.b, rw.b], [pb[0]])
                TT("dve", qh_.t[:], q_.t[:, 0:2, :], PS[:, 0:1, :].to_broadcast([128, 2, 512]), ALU.mult, [q_.b, pb[0]], [qh_.b])
                CP("act", eb_.t[:], PS[:, 0, :].rearrange("p (c t) -> p c t", t=LM)[:, :, LM - 1], [pb[0]], [eb_.b])
                MM(PS[:, 1, :], sel.t[:, hd, :], rw.t[:, 1, :], True, True, [sel.b, rw.b], [pb[1]])
                TT("dve", kh_.t[:], q_.t[:, 2:4, :], PS[:, 1:2, :].to_broadcast([128, 2, 512]), ALU.mult, [q_.b, pb[1]], [kh_.b])
                MM(PS[:, 0, :], sel.t[:, hd, :], rw.t[:, 2, :], True, True, [sel.b, rw.b], [pb[0]])
                TT("dve", kw2.t[:], q_.t[:, 2:4, :], PS[:, 0:1, :].to_broadcast([128, 2, 512]), ALU.mult, [q_.b, pb[0]], [kw2.b])
                for c in range(4):
                    for dk in range(2):
                        TR(PSB[:, 2, c * 256 + dk * 128:c * 256 + (dk + 1) * 128], kw2.t[:, dk, c * 128:(c + 1) * 128],
                           identb.t[:], [kw2.b, identb.b], [pb[2]])
                CP("act", kt_.t[:], PSB[:, 2, 0:1024].rearrange("p (c d) -> p c d", c=4), [pb[2]], [kt_.b])
            for c in range(4):
                cs = slice(c * LM, (c + 1) * LM)
                for hd in range(HM):
                    q_, v_, qh_, kh_, kt_, eb_, h_ = qk[hd][sl], vaug[hd][sl], qh[hd][sl], kh[hd][sl], ktok[hd][sl], ebl[hd][sl], hT[hd][sl]
                    u = n[0]
                    n[0] += 1
                    w_ = wT[u % 4]
                    sslot = u % 4
                    for dk in range(2):
                        MM(PS[:, 3, sslot * 128:(sslot + 1) * 128], kh_.t[:, dk, cs], qh_.t[:, dk, cs], dk == 0, dk == 1,
                           [kh_.b, qh_.b], [pb[3]])
                    TT("dve", w_.t[:], PS[:, 3, sslot * 128:(sslot + 1) * 128], tri.t[:], ALU.mult, [pb[3], tri.b], [w_.b])
                    ob = 4 + u % 2
                    for dv in range(2):
                        MM(PS[:, ob, dv * 128:(dv + 1) * 128], v_.t[:, c, dv * 128:(dv + 1) * 128], w_.t[:], True, False,
                           [v_.b, w_.b], [pb[ob]])
                        for dk in range(2):
                            MM(PS[:, ob, dv * 128:(dv + 1) * 128], Cbf.t[:, hd, dk, dv * 128:(dv + 1) * 128], qh_.t[:, dk, cs],
                               False, dk == 1, [Cbf.b, qh_.b], [pb[ob]])
                    MM(PS[:, ob, 256:384], v_.t[:, c, 256:384], w_.t[:], True, False, [v_.b, w_.b], [pb[ob]])
                    for dk in range(2):
                        MM(PS[:, ob, 256:384], Cbf.t[:, hd, dk, 256:384], qh_.t[:, dk, cs], False, dk == 1,
                           [Cbf.b, qh_.b], [pb[ob]])
                    for dk in range(2):
                        MM(PS[:, 6 + dk, 0:384], kt_.t[:, c, dk * 128:(dk + 1) * 128], v_.t[:, c, :], True, True,
                           [kt_.b, v_.b], [pb[6 + dk]])
                    for dk in range(2):
                        STT("dve", Cst.t[:, hd, dk, :], Cst.t[:, hd, dk, :], eb_.t[:, c:c + 1], PS[:, 6 + dk, 0:384],
                            ALU.mult, ALU.add, [Cst.b, eb_.b, pb[6 + dk]], [Cst.b])
                    a_ = aden[u % 2]
                    ACTF(a_.t[:], PS[:, ob, 256:384], AF.Abs, [pb[ob]], [a_.b])
                    TS("dve", a_.t[:], a_.t[:], 1.0, None, ALU.max, None, [a_.b], [a_.b])
                    RECIP(a_.t[:], a_.t[:], [a_.b], [a_.b])
                    TT("dve", h_.t[:, :, cs], PS[:, ob, 0:256].rearrange("p (a b) -> p a b", a=2),
                       a_.t[:, None, :].to_broadcast([128, 2, 128]), ALU.mult, [pb[ob], a_.b], [h_.b])
                    CP("act", Cbf.t[:, hd], Cst.t[:, hd], [Cst.b], [Cbf.b])
            for hd in range(HM):
                h_ = hT[hd][sl]
                u = n[0]
                n[0] += 1
                o_, s_, r_, hn_, ho_ = omb[u % 2], sqh[u % 2], rsh[u % 2], hn[u % 2], hob[u % 2]
                LD("sp", o_, o_.t[:], omT[2 * hd:2 * hd + 2, :, tok].rearrange("c p t -> p c t"), reads=[db("om", tb)])
                ACTF(s_.t[:], h_.t[:], AF.Square, [h_.b], [s_.b])
                bnk = u % 2
                for dv in range(2):
                    MM(PS[:, bnk, :], onesb.t[:], s_.t[:, dv, :], dv == 0, dv == 1, [onesb.b, s_.b], [pb[bnk]])
                ACTF(r_.t[:], PS[:, bnk, :], AF.Sqrt, [pb[bnk], epsT.b], [r_.b], bias=epsT.t[:, 0:1], scale=1.0 / 256)
                RECIP(r_.t[:], r_.t[:], [r_.b], [r_.b])
                for dv in range(2):
                    STT("dve", hn_.t[:, dv, :], h_.t[:, dv, :], mngT.t[:, 2 * hd + dv:2 * hd + dv + 1], r_.t[:],
                        ALU.mult, ALU.mult, [h_.b, mngT.b, r_.b], [hn_.b])
                TT("dve", ho_.t[:], hn_.t[:], o_.t[:], ALU.mult, [hn_.b, o_.b], [ho_.b])
                ST("sp", ho_, hmT[2 * hd:2 * hd + 2, :, tok].rearrange("c p t -> p c t"), ho_.t[:], pwrites=[db("hm", tb)])
        allb = [glf.b, glf2.b, gig.b, geb.b, geu.b, gew.b, Cst.b, Cbf.b] + [t.b for t in rowt]
        for grp in (qk, vaug, qh, kh, ktok, ebl, hT):
            for pr in grp:
                allb += [t.b for t in pr]
        allb += [t.b for t in kw_ + wT + aden + omb + sqh + rsh + hn + hob] + pb
        P.barrier(allb)
        ar.reset(m)

    def stage_merge(l):
        m = ar.mark()
        at = ar.alloc([128, 8, 1024], BF16, "at")
        hm_ = ar.alloc([128, 8, 1024], BF16, "hm")
        mg = ar.alloc([128, KC, 1024], BF16, "mg")
        wa = [ar.alloc([128, 8, 128], BF16) for _ in range(2)]
        wm = [ar.alloc([128, 8, 128], BF16) for _ in range(2)]
        wo = [ar.alloc([128, KC, 128], BF16) for _ in range(2)]
        ga = [ar.alloc([128, 512], BF16) for _ in range(2)]
        gm = [ar.alloc([128, 512], BF16) for _ in range(2)]
        t1 = [ar.alloc([128, 512], F32) for _ in range(2)]
        t2 = [ar.alloc([128, 512], F32) for _ in range(2)]
        xold = [ar.alloc([128, 512], F32) for _ in range(2)]
        xnew = [ar.alloc([128, 512], F32) for _ in range(2)]
        cnt = 0
        for tb2 in range(T // 1024):
            tk = slice(tb2 * 1024, (tb2 + 1) * 1024)
            LD("sp", at, at.t[:], attnT[:, :, tk].rearrange("h p t -> p h t"), reads=[db("attn", 2 * tb2), db("attn", 2 * tb2 + 1)])
            LD("sp", hm_, hm_.t[:], hmT[:, :, tk].rearrange("h p t -> p h t"), reads=[db("hm", 2 * tb2), db("hm", 2 * tb2 + 1)])
            for dc in range(KC):
                a_, m_ = wa[dc % 2], wm[dc % 2]
                LD("pool", a_, a_.t[:], w_up_a[l, :, dc * 128:(dc + 1) * 128].rearrange("(k p) n -> p k n", p=128))
                LD("pool", m_, m_.t[:], w_up_m[l, :, dc * 128:(dc + 1) * 128].rearrange("(k p) n -> p k n", p=128))
                for h in range(2):
                    tb = tb2 * 2 + h
                    ba = (cnt * 2) % 6
                    bm = ba + 1
                    g1, g2, u1, u2 = ga[cnt % 2], gm[cnt % 2], t1[cnt % 2], t2[cnt % 2]
                    cnt += 1
                    LD("sp", g1, g1.t[:], gaT[dc, :, tb * 512:(tb + 1) * 512], reads=[db("ga", tb)])
                    LD("sp", g2, g2.t[:], gmT[dc, :, tb * 512:(tb + 1) * 512], reads=[db("gm", tb)])
                    for k in range(8):
                        MM(PS[:, ba, :], a_.t[:, k, :], at.t[:, k, h * 512:(h + 1) * 512], k == 0, k == 7, [a_.b, at.b], [pb[ba]])
                    for k in range(8):
                        MM(PS[:, bm, :], m_.t[:, k, :], hm_.t[:, k, h * 512:(h + 1) * 512], k == 0, k == 7, [m_.b, hm_.b], [pb[bm]])
                    TT("dve", u1.t[:], PS[:, ba, :], g1.t[:], ALU.mult, [pb[ba], g1.b], [u1.b])
                    TT("dve", u2.t[:], PS[:, bm, :], g2.t[:], ALU.mult, [pb[bm], g2.b], [u2.b])
                    TT("dve", mg.t[:, dc, h * 512:(h + 1) * 512], u1.t[:], u2.t[:], ALU.add, [u1.b, u2.b], [mg.b])
            for d in range(KC):
                w = wo[d % 2]
                LD("pool", w, w.t[:], w_out[l, :, d * 128:(d + 1) * 128].rearrange("(k p) n -> p k n", p=128))
                for h in range(2):
                    bank = 6 + (cnt % 2)
                    for k in range(KC):
                        MM(PS[:, bank, :], w.t[:, k, :], mg.t[:, k, h * 512:(h + 1) * 512], k == 0, k == KC - 1,
                           [w.b, mg.b], [pb[bank]])
                    resid_epilogue(1, d, tb2 * 2 + h, bank, xold[cnt % 2], xnew[cnt % 2])
                    cnt += 1
        allb = [at.b, hm_.b, mg.b] + [t.b for t in wa + wm + wo + ga + gm + t1 + t2 + xold + xnew] + pb
        P.barrier(allb)
        ar.reset(m)

    def stage_out():
        m = ar.mark()
        xi = [ar.alloc([128, KC, 128], F32) for _ in range(2)]
        xo = [ar.alloc([128, D], F32) for _ in range(2)]
        ob = Buf("outbuf")
        for tt in range(T // 128):
            a = xi[tt % 2]
            o = xo[tt % 2]
            LD("sp", a, a.t[:], xT[:, :, tt * 128:(tt + 1) * 128].rearrange("k p t -> p k t"), reads=xbufs(tt // 4))
            for q in range(4):
                bank = (tt * 4 + q) % 8
                for i in range(4):
                    k = q * 4 + i
                    TR(PS[:, bank, i * 128:(i + 1) * 128], a.t[:, k, :], ident.t[:], [a.b, ident.b], [pb[bank]])
                ceng = "act" if q % 2 else "dve"
                cin = PS[:, bank, :]
                cout = o.t[:, q * 512:(q + 1) * 512]
                P.op(ceng, (lambda e, cout=cout, cin=cin: e.copy(out=cout, in_=cin)) if ceng == "act" else
                     (lambda e, cout=cout, cin=cin: e.tensor_copy(out=cout, in_=cin)), [pb[bank]],
                     [o.b] if q == 0 else [], [] if q == 0 else [o.b])
            ST("sp", o, out[tt * 128:(tt + 1) * 128, :], o.t[:], pwrites=[ob])
        P.op("sp", None, reads=[ob])
        ar.reset(m)

    seq = [("in", None)]
    for l in range(depth):
        seq += [("mod", l), ("ffn1", l), ("proj", l), ("attn", l), ("mlstm", l), ("merge", l), ("ffn2", l)]
    for name, l in seq:
        if name == "in":
            stage_in()
        elif name == "mod":
            stage_mod(l)
        elif name == "ffn1":
            stage_ffn(l, 0)
        elif name == "proj":
            stage_proj(l)
        elif name == "attn":
            stage_attn(l)
        elif name == "mlstm":
            stage_mlstm(l)
        elif name == "merge":
            stage_merge(l)
        elif name == "ffn2":
            stage_ffn(l, 2)
        if stop is not None and (name, l if l is not None else 0) == stop:
            break
    stage_out()
    fence_bufs = [b for b in dbufs.values()]
    P.op("sp", None, reads=fence_bufs)
    with es:
        info = P.emit()
    return nc, info, ar.peak


def _pk(v):
    v = np.asarray(v)
    return np.ascontiguousarray(np.swapaxes(v.reshape(v.shape[:-1] + (v.shape[-1] // 128, 128)), -1, -2))


def _rel_index():
    kp = np.arange(128)[:, None, None]
    kt = np.arange(5)[None, :, None]
    q = np.arange(128)[None, None, :]
    rel = 512 + q - kt * 128 - kp
    idx = np.clip(rel, -63, 256) + 63
    qc = q // 64
    kc = (kt * 128 + kp) // 64
    valid = (kc >= qc) & (kc <= qc + 8)
    return idx, valid


def prep_inputs(inputs, b):
    f = lambda a: np.ascontiguousarray(np.asarray(a, dtype=np.float32))
    idx, valid = _rel_index()
    rel = np.asarray(inputs["rel_table"], dtype=np.float32)
    relb = rel[:, :, idx]
    relb = np.ascontiguousarray(np.transpose(relb, (0, 2, 1, 3, 4)))
    amask = np.where(valid, np.float32(0.0), np.float32(-30000.0)).astype(np.float32)
    amask = np.ascontiguousarray(np.broadcast_to(amask, (128, 5, 128)))
    bifv = np.asarray(inputs["b_if"], dtype=np.float32)
    bif = np.ascontiguousarray(np.stack([bifv[:, :4], bifv[:, 4:]], axis=-1))
    convw = np.asarray(inputs["conv_w"], dtype=np.float32)
    convw = np.ascontiguousarray(np.transpose(convw.reshape(2, 4, 16, 128), (0, 3, 2, 1)))
    qkg = np.ascontiguousarray(np.stack([np.asarray(inputs["q_norm_g"]), np.asarray(inputs["k_norm_g"])], axis=-1).astype(np.float32))
    m = {
        "x": f(inputs["x"][b]),
        "cvec": _pk(f(inputs["c"][b])),
        "normg": _pk(f(inputs["norm_g"])),
        "w_ada": f(inputs["w_ada"]),
        "b_ada": np.ascontiguousarray(np.swapaxes(f(inputs["b_ada"]).reshape(2, 144, 128), 1, 2)),
        "ffn1_w1": f(inputs["ffn1_w1"]), "ffn1_w3": f(inputs["ffn1_w3"]), "ffn1_w2": f(inputs["ffn1_w2"]),
        "ffn2_w1": f(inputs["ffn2_w1"]), "ffn2_w3": f(inputs["ffn2_w3"]), "ffn2_w2": f(inputs["ffn2_w2"]),
        "w_in": f(inputs["w_in"]),
        "bif": bif,
        "convw": convw,
        "convb": _pk(f(inputs["conv_b"])),
        "qkg": qkg,
        "relb": relb,
        "amask": amask,
        "mng": _pk(f(inputs["m_norm_g"])),
        "w_up_a": f(inputs["w_up_a"]), "w_up_m": f(inputs["w_up_m"]), "w_out": f(inputs["w_out"]),
    }
    return m


_CACHE = {}
ACTIVE = (0, 2, 4, 6)


def kernel(**inputs):
    if "nc" not in _CACHE:
        _CACHE["nc"] = build()[0]
    nc = _CACHE["nc"]
    shared = prep_inputs(inputs, 0)
    zeros = {k: np.zeros_like(v) for k, v in shared.items()}
    in_maps = []
    for core in range(8):
        if core in ACTIVE:
            b = ACTIVE.index(core)
            m = dict(shared)
            m["x"] = np.ascontiguousarray(np.asarray(inputs["x"][b], dtype=np.float32))
            m["cvec"] = _pk(np.asarray(inputs["c"][b], dtype=np.float32))
        else:
            m = zeros
        in_maps.append(m)
    res = run_bass_kernel_spmd(nc, in_maps, core_ids=list(range(8)))
    outs = [res.results[ACTIVE[b]]["out"] for b in range(4)]
    return np.stack(outs, axis=0).astype(np.float32)
```
